# Optimizing a Trainium2 kernel written in Bass

```python
import math
import jax, jax.numpy as jnp
from jax import lax
import numpy as np

D_MODEL = 1024
BATCH = 8
SEQ = 4096
DEPTH = 4

HEAD_DIM = 64
SWA_Q_HEADS = 8
SWA_KV_HEADS = 2
SWA_GROUP = SWA_Q_HEADS // SWA_KV_HEADS
WINDOW = 128
FOX_HEADS = 4
MLA_HEADS = 4
MLA_Q_RANK = 256
MLA_KV_RANK = 128
MLA_NOPE_DIM = 64
MLA_ROPE_DIM = 32
MLA_V_DIM = 64
ROPE_THETA = 10000.0
REL_BUCKETS = 32
REL_MAX_DIST = 128
D_FF = 2816
CONV_WIDTH = 3
Q_BLOCK = 128
EPS = 1e-6
NEG_INF = -1e30

SWA_WIDTH = SWA_Q_HEADS * HEAD_DIM
FOX_WIDTH = FOX_HEADS * HEAD_DIM
MLA_WIDTH = MLA_HEADS * MLA_V_DIM
MIX_WIDTH = SWA_WIDTH + FOX_WIDTH + MLA_WIDTH
SWA_COLS = (SWA_Q_HEADS + 2 * SWA_KV_HEADS) * HEAD_DIM
FOX_COLS = 3 * FOX_HEADS * HEAD_DIM + FOX_HEADS
MLA_COLS = MLA_Q_RANK + MLA_KV_RANK + MLA_ROPE_DIM
IN_COLS = SWA_COLS + FOX_COLS + MLA_COLS
MLA_QK_DIM = MLA_NOPE_DIM + MLA_ROPE_DIM

kernel_name = "hymba_swa_fox_mla_convffn_trunk"


def rmsnorm(x, g):
    xf = x.astype(jnp.float32)
    y = xf * lax.rsqrt(jnp.mean(xf * xf, axis=-1, keepdims=True) + EPS) * g.astype(jnp.float32)
    return y.astype(x.dtype)


def t5_causal_bucket(dist):
    max_exact = REL_BUCKETS // 2
    d = jnp.maximum(dist, 0)
    log_ratio = jnp.log(jnp.maximum(d, 1).astype(jnp.float32) / max_exact) / math.log(REL_MAX_DIST / max_exact)
    large = max_exact + (log_ratio * (REL_BUCKETS - max_exact)).astype(jnp.int32)
    large = jnp.minimum(large, REL_BUCKETS - 1)
    return jnp.where(d < max_exact, d, large)


def apply_rope(t, cos, sin):
    t1, t2 = jnp.split(t, 2, axis=-1)
    return jnp.concatenate([t1 * cos - t2 * sin, t1 * sin + t2 * cos], axis=-1)


def swa_sink_attention(q, k, v, sinks, rel_bias):
    B, S = q.shape[0], q.shape[1]
    nb = S // WINDOW
    qb = q.reshape(B, nb, WINDOW, SWA_KV_HEADS, SWA_GROUP, HEAD_DIM)

    def band(t):
        tb = t.reshape(B, nb, WINDOW, SWA_KV_HEADS, HEAD_DIM)
        prev = jnp.pad(tb, ((0, 0), (1, 0), (0, 0), (0, 0), (0, 0)))[:, :-1]
        return jnp.concatenate([prev, tb], axis=2)

    kb, vb = band(k), band(v)
    qi = jnp.arange(WINDOW, dtype=jnp.int32)[:, None] + WINDOW
    kj = jnp.arange(2 * WINDOW, dtype=jnp.int32)[None, :]
    dist = qi - kj
    in_band = (dist >= 0) & (dist < WINDOW)
    valid_key = (jnp.arange(nb)[:, None, None] > 0) | (kj >= WINDOW)[None]
    mask = in_band[None] & valid_key
    bias = rel_bias.astype(jnp.float32)[t5_causal_bucket(dist)]
    bias = bias.transpose(2, 0, 1).reshape(SWA_KV_HEADS, SWA_GROUP, WINDOW, 2 * WINDOW)
    s = jnp.einsum('bnqhgd,bnkhd->bnhgqk', qb, kb, preferred_element_type=jnp.float32)
    s = s * (HEAD_DIM ** -0.5) + bias
    s = jnp.where(mask[None, :, None, None], s, NEG_INF)
    sink = sinks.astype(jnp.float32).reshape(1, 1, SWA_KV_HEADS, SWA_GROUP, 1, 1)
    sink = jnp.broadcast_to(sink, s.shape[:-1] + (1,))
    p = jax.nn.softmax(jnp.concatenate([s, sink], axis=-1), axis=-1)[..., :-1]
    o = jnp.einsum('bnhgqk,bnkhd->bnqhgd', p.astype(v.dtype), vb)
    return o.reshape(B, S, SWA_Q_HEADS * HEAD_DIM)


def blocked_causal_attention(q, k, v, scale, log_forget_cum=None):
    B, S, H = q.shape[0], q.shape[1], q.shape[2]
    nb = S // Q_BLOCK
    q_blocks = q.reshape(B, nb, Q_BLOCK, H, q.shape[-1]).swapaxes(0, 1)
    k_pos = jnp.arange(S, dtype=jnp.int32)
    if log_forget_cum is None:
        f_blocks, f_keys = None, None
    else:
        f_t = log_forget_cum.transpose(0, 2, 1)
        f_blocks = f_t.reshape(B, H, nb, Q_BLOCK).transpose(2, 0, 1, 3)
        f_keys = f_t

    def one_block(args):
        qb, fb, i = args
        s = jnp.einsum('bqhd,bkhd->bhqk', qb, k, preferred_element_type=jnp.float32) * scale
        if fb is not None:
            s = s + (fb[..., :, None] - f_keys[..., None, :])
        q_pos = i * Q_BLOCK + jnp.arange(Q_BLOCK, dtype=jnp.int32)
        s = jnp.where(k_pos[None, :] <= q_pos[:, None], s, NEG_INF)
        p = jax.nn.softmax(s, axis=-1)
        return jnp.einsum('bhqk,bkhd->bqhd', p.astype(v.dtype), v)

    out = lax.map(one_block, (q_blocks, f_blocks, jnp.arange(nb, dtype=jnp.int32)))
    return out.swapaxes(0, 1).reshape(B, S, H * v.shape[-1])


def causal_depthwise_conv(u, w, b):
    S = u.shape[1]
    up = jnp.pad(u, ((0, 0), (CONV_WIDTH - 1, 0), (0, 0)))
    y = b.astype(u.dtype)
    for tap in range(CONV_WIDTH):
        y = y + w[tap].astype(u.dtype) * up[:, tap:tap + S]
    return y


def setup_inputs(seed: int = 0) -> dict:
    key = jax.random.key(seed)
    ks = jax.random.split(key, 20)
    L, D = DEPTH, D_MODEL
    f32 = jnp.float32

    def nrm(k, shape, scale):
        return jax.random.normal(k, shape, f32) * scale

    def gain(k, shape):
        return 1.0 + 0.05 * jax.random.normal(k, shape, f32)

    return {
        "x": jax.random.normal(ks[0], (BATCH, SEQ, D), f32),
        "attn_pre_norm": gain(ks[1], (L, D)),
        "w_in": nrm(ks[2], (L, D, IN_COLS), D ** -0.5),
        "forget_bias": 2.0 + 0.5 * jax.random.normal(ks[3], (L, FOX_HEADS), f32),
        "swa_sinks": nrm(ks[4], (L, SWA_Q_HEADS), 0.5),
        "rel_bias": nrm(ks[5], (REL_BUCKETS, SWA_Q_HEADS), 0.5),
        "q_latent_norm": gain(ks[6], (L, MLA_Q_RANK)),
        "w_uq": nrm(ks[7], (L, MLA_Q_RANK, MLA_HEADS * MLA_QK_DIM), MLA_Q_RANK ** -0.5),
        "kv_latent_norm": gain(ks[8], (L, MLA_KV_RANK)),
        "w_ukv": nrm(ks[9], (L, MLA_KV_RANK, MLA_HEADS * (MLA_NOPE_DIM + MLA_V_DIM)), MLA_KV_RANK ** -0.5),
        "group_norm": gain(ks[10], (L, MIX_WIDTH)),
        "w_out": nrm(ks[11], (L, MIX_WIDTH, D), MIX_WIDTH ** -0.5),
        "attn_post_norm": gain(ks[12], (L, D)),
        "ffn_pre_norm": gain(ks[13], (L, D)),
        "w_up": nrm(ks[14], (L, D, 2 * D_FF), D ** -0.5),
        "conv_w": nrm(ks[15], (L, CONV_WIDTH, 2 * D_FF), CONV_WIDTH ** -0.5),
        "conv_b": nrm(ks[16], (L, 2 * D_FF), 0.02),
        "w_down": nrm(ks[17], (L, D_FF, D), D_FF ** -0.5),
        "ffn_post_norm": gain(ks[18], (L, D)),
    }


def reference(x, attn_pre_norm, w_in, forget_bias, swa_sinks, rel_bias, q_latent_norm, w_uq,
              kv_latent_norm, w_ukv, group_norm, w_out, attn_post_norm, ffn_pre_norm, w_up,
              conv_w, conv_b, w_down, ffn_post_norm):
    B, S, _ = x.shape
    pos = jnp.arange(S, dtype=jnp.float32)
    inv_freq = ROPE_THETA ** (-(jnp.arange(MLA_ROPE_DIM // 2, dtype=jnp.float32) * 2.0 / MLA_ROPE_DIM))
    ang = pos[:, None] * inv_freq[None, :]
    cos = jnp.cos(ang)[:, None, :].astype(x.dtype)
    sin = jnp.sin(ang)[:, None, :].astype(x.dtype)

    for l in range(DEPTH):
        h = rmsnorm(x, attn_pre_norm[l])
        proj = h @ w_in[l]
        a_cols, f_cols, m_cols = jnp.split(proj, [SWA_COLS, SWA_COLS + FOX_COLS], axis=-1)

        qa, ka, va = jnp.split(a_cols, [SWA_WIDTH, SWA_WIDTH + SWA_KV_HEADS * HEAD_DIM], axis=-1)
        out_a = swa_sink_attention(qa.reshape(B, S, SWA_Q_HEADS, HEAD_DIM),
                                   ka.reshape(B, S, SWA_KV_HEADS, HEAD_DIM),
                                   va.reshape(B, S, SWA_KV_HEADS, HEAD_DIM),
                                   swa_sinks[l], rel_bias)

        qf, kf, vf, f_logit = jnp.split(f_cols, [FOX_WIDTH, 2 * FOX_WIDTH, 3 * FOX_WIDTH], axis=-1)
        log_f = jax.nn.log_sigmoid(f_logit.astype(jnp.float32) + forget_bias[l].astype(jnp.float32))
        F = jnp.cumsum(log_f, axis=1)
        out_b = blocked_causal_attention(qf.reshape(B, S, FOX_HEADS, HEAD_DIM),
                                         kf.reshape(B, S, FOX_HEADS, HEAD_DIM),
                                         vf.reshape(B, S, FOX_HEADS, HEAD_DIM),
                                         HEAD_DIM ** -0.5, log_forget_cum=F)

        c_q, c_kv, k_rope = jnp.split(m_cols, [MLA_Q_RANK, MLA_Q_RANK + MLA_KV_RANK], axis=-1)
        qm = (rmsnorm(c_q, q_latent_norm[l]) @ w_uq[l]).reshape(B, S, MLA_HEADS, MLA_QK_DIM)
        q_nope, q_rope = jnp.split(qm, [MLA_NOPE_DIM], axis=-1)
        kv = (rmsnorm(c_kv, kv_latent_norm[l]) @ w_ukv[l]).reshape(B, S, MLA_HEADS, MLA_NOPE_DIM + MLA_V_DIM)
        k_nope, vm = jnp.split(kv, [MLA_NOPE_DIM], axis=-1)
        q_rope = apply_rope(q_rope, cos, sin)
        k_rope = jnp.broadcast_to(apply_rope(k_rope[:, :, None, :], cos, sin), (B, S, MLA_HEADS, MLA_ROPE_DIM))
        out_c = blocked_causal_attention(jnp.concatenate([q_nope, q_rope], axis=-1),
                                         jnp.concatenate([k_nope, k_rope], axis=-1),
                                         vm, MLA_QK_DIM ** -0.5)

        g_a, g_b, g_c = jnp.split(group_norm[l], [SWA_WIDTH, SWA_WIDTH + FOX_WIDTH])
        mixed = jnp.concatenate([rmsnorm(out_a, g_a), rmsnorm(out_b, g_b), rmsnorm(out_c, g_c)], axis=-1)
        x = x + rmsnorm(mixed @ w_out[l], attn_post_norm[l])

        h = rmsnorm(x, ffn_pre_norm[l])
        u = causal_depthwise_conv(h @ w_up[l], conv_w[l], conv_b[l])
        gate, up = jnp.split(u, 2, axis=-1)
        y = (jax.nn.gelu(gate, approximate=True) * up) @ w_down[l]
        x = x + rmsnorm(y, ffn_post_norm[l])
    return x
```

```python
import math
import numpy as np
from contextlib import ExitStack
import concourse.bass as bass
import concourse.mybir as mybir
from concourse.bass_utils import run_bass_kernel_spmd

F32 = mybir.dt.float32
BF16 = mybir.dt.bfloat16
AF = mybir.ActivationFunctionType
ALU = mybir.AluOpType

D = 1024
DEPTH = 4
SEQ = 4096
NPAR = 60 + 176
EPS = 1e-6
W1C = 1668
QS, KD, TV, FQ, FK, FF = 0, 512, 768, 1152, 1408, 1664
W2C = 576
DFF = 2816
FB = 256

ENGS = ["pe", "act", "dve", "pool", "sp"]


class Buf:
    __slots__ = ("name", "w", "r", "dcnt", "key", "excl")

    def __init__(self, name):
        self.name = name
        self.excl = False
        self.w = None
        self.r = {}
        self.dcnt = 0
        self.key = ("dma", name)


class Tl:
    def __init__(self, t, name):
        self.t = t
        self.b = Buf(name)

    def __getitem__(self, idx):
        return self.t[idx]


def _b(x):
    return x.b if isinstance(x, Tl) else x


class Sched:
    def __init__(self):
        self.ops = {e: [] for e in ENGS}
        self.cnt = {e: 0 for e in ENGS}
        self.seen = {e: {} for e in ENGS}
        self.dma_bufs = {}

    def _deps(self, eng, reads, writes, own_key=None):
        need = {}
        for b in reads:
            if b.w is not None:
                k, v = b.w
                if need.get(k, 0) < v:
                    need[k] = v
            if b.excl:
                for k, v in b.r.items():
                    if k != eng and need.get(k, 0) < v:
                        need[k] = v
        for b in writes:
            if b.w is not None:
                k, v = b.w
                if need.get(k, 0) < v:
                    need[k] = v
            for k, v in b.r.items():
                if need.get(k, 0) < v:
                    need[k] = v
        waits = []
        seen = self.seen[eng]
        for k, v in need.items():
            if k == eng and eng == "pe":
                continue
            if own_key is not None and k == own_key:
                continue
            if k in self.dma_bufs:
                v = self.dma_bufs[k].dcnt
            if seen.get(k, 0) >= v:
                continue
            seen[k] = v
            waits.append((k, v))
        return waits

    def op(self, eng, fn, reads=(), writes=()):
        reads = [_b(x) for x in reads]
        writes = [_b(x) for x in writes]
        waits = self._deps(eng, reads, writes)
        self.cnt[eng] += 1
        v = self.cnt[eng]
        for b in reads:
            if b.r.get(eng, 0) < v:
                b.r[eng] = v
        for b in writes:
            b.w = (eng, v)
            b.r = {}
        self.ops[eng].append((waits, fn, eng, 1))

    def dma(self, q, fn, dst, reads=()):
        dst = _b(dst)
        reads = [_b(x) for x in reads]
        key = dst.key
        self.dma_bufs[key] = dst
        waits = self._deps(q, reads, [dst], own_key=key)
        dst.dcnt += 16
        for b in reads:
            if b.r.get(key, 0) < dst.dcnt:
                b.r[key] = dst.dcnt
        dst.w = (key, dst.dcnt)
        dst.r = {}
        self.ops[q].append((waits, fn, key, 16))

    def barrier(self):
        tot = {e: self.cnt[e] for e in ("pe", "act", "dve", "pool")}
        for k, b in self.dma_bufs.items():
            tot[k] = b.dcnt
        for e in ENGS:
            waits = []
            for k, v in tot.items():
                if k == e or v == 0:
                    continue
                if self.seen[e].get(k, 0) >= v:
                    continue
                self.seen[e][k] = v
                waits.append((k, v))
            if waits:
                self.ops[e].append((waits, None, None, 0))

    def wait_all(self, eng, bufs):
        waits = self._deps(eng, [_b(x) for x in bufs], [])
        self.ops[eng].append((waits, None, None, 0))

    def emit_one(self, eng, name, sems):
        for waits, fn, key, inc in self.ops[name]:
            for k, v in waits:
                eng.wait_ge(sems[k], v)
            if fn is not None:
                fn(eng).then_inc(sems[key], inc)


class KB:
    def __init__(self, nc, nblk, depth):
        self.nc = nc
        self.S = Sched()
        self.uid = 0
        self.nblk = nblk
        self.depth = depth
        self.rot = {}

    def sb(self, es, name, shape, dt):
        self.uid += 1
        nm = f"{name}_{self.uid}"
        return Tl(es.enter_context(self.nc.sbuf_tensor(nm, shape, dt)), nm)

    def nxt(self, key, lst):
        i = self.rot.get(key, 0)
        self.rot[key] = i + 1
        return lst[i % len(lst)]

    def mm(self, out, lhsT, rhs, start, stop, R, W):
        self.S.op("pe", lambda e: e.matmul(out, lhsT=lhsT, rhs=rhs, start=start, stop=stop), R, W)

    def act(self, out, in_, func, R, W, scale=None, bias=None):
        kw = {}
        if scale is not None:
            kw["scale"] = scale
        if bias is not None:
            kw["bias"] = bias
        self.S.op("act", lambda e: e.activation(out=out, in_=in_, func=func, **kw), R, W)

    def tt(self, eng, out, in0, in1, op, R, W):
        self.S.op(eng, lambda e: e.tensor_tensor(out=out, in0=in0, in1=in1, op=op), R, W)

    def ts(self, eng, out, in0, s1, s2, op0, op1, R, W):
        if op1 is None:
            self.S.op(eng, lambda e: e.tensor_scalar(out=out, in0=in0, scalar1=s1, scalar2=None, op0=op0), R, W)
        else:
            self.S.op(eng, lambda e: e.tensor_scalar(out=out, in0=in0, scalar1=s1, scalar2=s2, op0=op0, op1=op1), R, W)

    def stt(self, out, in0, scalar, in1, op0, op1, R, W):
        self.S.op("dve", lambda e: e.scalar_tensor_tensor(out=out, in0=in0, scalar=scalar, in1=in1, op0=op0, op1=op1), R, W)

    def cp(self, eng, out, in_, R, W):
        if eng == "act":
            self.S.op("act", lambda e: e.activation(out=out, in_=in_, func=AF.Copy), R, W)
        else:
            self.S.op(eng, lambda e: e.tensor_copy(out=out, in_=in_), R, W)

    def memset(self, eng, ap, val, W):
        self.S.op(eng, lambda e: e.memset(ap, val), (), W)

    def dma(self, q, out, in_, dst, R=()):
        self.S.dma(q, lambda e: e.dma_start(out=out, in_=in_), dst, R)

    def rms_sq(self, srcs, bufs, P, n, sqt, engs=("pool", "dve", "act", "pool", "dve", "pool", "dve", "act")):
        for j, (ap, b) in enumerate(zip(srcs, bufs)):
            sq = sqt[j]
            en = engs[j % len(engs)]
            if en == "act":
                self.act(sq[0:P, 0:n], ap, AF.Square, [b], [sq])
            else:
                self.tt(en, sq[0:P, 0:n], ap, ap, ALU.mult, [b], [sq])

    def rms_fin(self, nsrc, P, Dn, n, sqt):
        c = self.c
        ps = c["ps_stat"]
        for j in range(nsrc):
            sq = sqt[j]
            self.mm(ps[:, 0:n], c["ones_bf"][0:P, 0:128], sq[0:P, 0:n], j == 0, j == nsrc - 1,
                    [sq, c["ones_bf"]], [ps])
        lnv = c["lnv"]
        rstd = self.nxt("rstd", c["rstd"])
        self.act(lnv[:, 0:n], ps[:, 0:n], AF.Ln, [ps, c["epsb"]], [lnv], scale=1.0 / Dn, bias=c["epsb"][:, 0:1])
        self.act(rstd[:, 0:n], lnv[:, 0:n], AF.Exp, [lnv], [rstd], scale=-0.5)
        return rstd

    def rms(self, srcs, bufs, P, Dn, n, engs=("pool", "dve", "act", "pool", "dve", "pool", "dve", "act")):
        c = self.c
        ps = c["ps_stat"]
        for j, (ap, b) in enumerate(zip(srcs, bufs)):
            sq = self.nxt("sq", c["sq"])
            en = engs[j % len(engs)]
            if en == "act":
                self.act(sq[0:P, 0:n], ap, AF.Square, [b], [sq])
            else:
                self.tt(en, sq[0:P, 0:n], ap, ap, ALU.mult, [b], [sq])
            self.mm(ps[:, 0:n], c["ones_bf"][0:P, 0:128], sq[0:P, 0:n], j == 0, j == len(srcs) - 1,
                    [sq, c["ones_bf"]], [ps])
        lnv = c["lnv"]
        rstd = self.nxt("rstd", c["rstd"])
        self.act(lnv[:, 0:n], ps[:, 0:n], AF.Ln, [ps, c["epsb"]], [lnv], scale=1.0 / Dn, bias=c["epsb"][:, 0:1])
        self.act(rstd[:, 0:n], lnv[:, 0:n], AF.Exp, [lnv], [rstd], scale=-0.5)
        return rstd

    def proj_fm(self, w, col0, M, hT, n=512):
        ps = self.nxt("pp", self.c["ps_proj"])
        for k in range(8):
            self.mm(ps[0:M, 0:n], w[:, k, col0:col0 + M], hT[k][:, 0:n], k == 0, k == 7, [w, hT[k]], [ps])
        return ps

    def load_x(self, src, c0, n, xblk, srcB=()):
        self.dma("sp", xblk[:, :, 0:n], src.rearrange("(k p) s -> p k s", p=128)[:, :, c0:c0 + n], xblk, srcB)

    def prenorm(self, src, c0, n, xblk, hT, par, gcol, off=0, load=True):
        if load:
            self.load_x(src, c0, n, xblk)
        rstd = self.rms([xblk[:, k, 0:n] for k in range(8)], [xblk] * 8, 128, D, n)
        for k in range(8):
            self.stt(hT[k][:, off:off + n], xblk[:, k, 0:n], par[:, gcol + k:gcol + k + 1], rstd[:, 0:n],
                     ALU.mult, ALU.mult, [xblk, par, rstd], [hT[k]])

    def attn_family(self, i, kcs, qts, vc, KD_, scale, grp, hooks=None):
        c = self.c
        ntile = 4 * i + 4
        tasks = [(h, j) for h in range(4) for j in range(ntile)]
        T = len(tasks)
        L = 4
        st = {}
        pso = {}
        deferred = []
        for t in range(T + L + 5):
            if L <= t < T + L:
                h, j = tasks[t - L]
                ps_s, q0 = st.pop(t - L)
                if j == 0:
                    pso[h] = self.nxt("po", c["ps_o"])
                ps_o = pso[h]
                pt = self.nxt("pt", c["pt"])
                self.act(pt[:, q0:512], ps_s[:, q0:512], AF.Exp, [ps_s], [pt], scale=scale)
                self.mm(ps_o[0:65, q0:512], vc[:, j, h, 0:65], pt[:, q0:512], j == 0, j == ntile - 1, [vc, pt], [ps_o])
                if j == ntile - 1:
                    rd = self.finish_head(ps_o, None, 512)

                    def fin(ps_o=ps_o, rd=rd, h=h):
                        osb, ps_bc = self.finish_norm(ps_o, rd, 512)
                        self.tt("dve", grp[0:64, h, :], osb[0:64, :], ps_bc[0:64, :], ALU.mult, [osb, ps_bc], [grp])
                    deferred.append((t + 3, fin))
            if t < T:
                h, j = tasks[t]
                d = j - 4 * i
                q0 = 128 * d if d > 0 else 0
                ps_s = self.nxt("pss", c["ps_s"])
                self.mm(ps_s[:, q0:512], kcs[h][0:KD_, j * 128:(j + 1) * 128], qts[h][0:KD_, q0:512], True, d < 0,
                        [kcs[h], qts[h]], [ps_s])
                if d >= 0:
                    self.mm(ps_s[:, q0:q0 + 128], c["ident"][:, :], c["maskT"][:, :], False, True,
                            [c["ident"], c["maskT"]], [ps_s])
                st[t] = (ps_s, q0)
            while deferred and deferred[0][0] <= t:
                deferred.pop(0)[1]()
            if hooks and t in hooks:
                hooks[t]()
        assert not deferred and not st

    def finish_head(self, ps_o, sink_ap, n, col0=0, rd=None):
        c = self.c
        lnd = c["lnd"]
        if rd is None:
            rd = self.nxt("rd", c["rd"])
        sl = slice(col0, col0 + n)
        if sink_ap is None:
            self.act(lnd[64:65, sl], ps_o[64:65, sl], AF.Ln, [ps_o], [lnd])
        else:
            self.act(lnd[64:65, sl], ps_o[64:65, sl], AF.Ln, [ps_o, c["esink"]], [lnd], bias=sink_ap)
        self.act(rd[64:65, sl], lnd[64:65, sl], AF.Exp, [lnd], [rd], scale=-1.0)
        return rd

    def finish_norm(self, ps_o, rd, n, cpeng="dve"):
        c = self.c
        ps_bc = c["ps_bc"]
        rb = self.nxt("rdb", c["rdb"])
        self.cp("dve", rb[64:65, 0, 0:n], rd[64:65, 0:n], [rd], [rb])
        self.stt(rb[64:65, 1, 0:n], rb[64:65, 0, 0:n], -1.0, rd[64:65, 0:n], ALU.mult, ALU.add, [rb, rd], [rb])
        self.mm(ps_bc[0:64, 0:n], c["sel64"][:, 0:64], rb[:, 0, 0:n], True, False, [c["sel64"], rb], [ps_bc])
        self.mm(ps_bc[0:64, 0:n], c["sel64"][:, 0:64], rb[:, 1, 0:n], False, True, [c["sel64"], rb], [ps_bc])
        osb = self.nxt("osb", c["osb"])
        self.cp(cpeng, osb[0:64, 0:n], ps_o[0:64, 0:n], [ps_o], [osb])
        return osb, ps_bc

    def evac(self, out, in_, R, W):
        k = self.rot.get("evac", 0)
        self.rot["evac"] = k + 1
        self.cp("act" if k % 4 == 3 else "dve", out, in_, R, W)


class _Stop(Exception):
    pass


class CleanStack(ExitStack):
    def __exit__(self, *exc):
        super().__exit__(None, None, None)
        return False


def build_program(nblk=8, depth=DEPTH, upto=None):
    S_ = nblk * 512
    NT = S_ // 128
    nc = bass.Bass("TRN2", target_bir_lowering=False)

    def din(name, shape, dt=F32):
        return nc.dram_tensor(name, shape, dt, kind="ExternalInput").ap()

    xT = din("xT", [D, S_])
    wa1 = din("wa1", [depth, D, W1C])
    wa2 = din("wa2", [depth, D, W2C])
    wuq = din("wuq", [depth, 256, 768])
    wukv = din("wukv", [depth, 128, 512])
    wout = din("wout", [depth, D, D])
    wup = din("wup", [depth, D, 2 * DFF])
    wdn = din("wdn", [depth, DFF, D])
    params = din("params", [depth, 128, NPAR])
    c_ident = din("c_ident", [128, 128])
    c_mask = din("c_mask", [128, 128])
    c_sel = din("c_sel", [4, 12 * 128])
    c_rope = din("c_rope", [2, 128, S_])
    c_biasg = din("c_biasg", [128, 2048])
    c_m01 = din("c_m01", [128, 2048])
    outT = nc.dram_tensor("outT", [D, S_], F32, kind="ExternalOutput").ap()
    xa = nc.dram_tensor("xa", [D, S_], F32, kind="Internal").ap()
    xb = nc.dram_tensor("xb", [D, S_], F32, kind="Internal").ap()
    mixab = nc.dram_tensor("mixab", [768, S_], BF16, kind="Internal").ap()
    B_xa, B_xb, B_mix, B_out = Buf("xa"), Buf("xb"), Buf("mixab"), Buf("outT")

    kb = KB(nc, nblk, depth)
    S = kb.S
    with CleanStack() as top:
        c = kb.c = {}
        c["ones_bf"] = kb.sb(top, "ones_bf", [128, 128], BF16)
        c["ones_f"] = kb.sb(top, "ones_f", [128, 128], F32)
        c["ident"] = kb.sb(top, "ident", [128, 128], BF16)
        c["maskT"] = kb.sb(top, "maskT", [128, 128], BF16)
        c["epsb"] = kb.sb(top, "epsb", [128, 2], F32)
        c["sq"] = [kb.sb(top, "sq", [128, 512], BF16) for _ in range(3)]
        c["lnv"] = kb.sb(top, "lnv", [128, 512], F32)
        c["rstd"] = [kb.sb(top, "rstd", [128, 512], F32) for _ in range(2)]
        c["esink"] = kb.sb(top, "esink", [128, 8], F32)
        pst = [Tl(top.enter_context(nc.psum_tensor(f"psum{j}", [128, 512], F32)), f"psum{j}") for j in range(8)]
        for p_ in pst:
            p_.b.excl = True
        c["ps_stat"] = pst[0]
        c["ps_proj"] = [pst[1], pst[2]]
        c["ps_s"] = [pst[3], pst[4], pst[1], pst[2]]
        c["ps_o"] = [pst[5], pst[6]]
        c["ps_bc"] = pst[7]

        kb.memset("dve", c["ones_bf"][:, :], 1.0, [c["ones_bf"]])
        c["sel64"] = kb.sb(top, "sel64", [128, 64], BF16)
        kb.memset("dve", c["sel64"][:, :], 0.0, [c["sel64"]])
        kb.memset("dve", c["sel64"][64:65, :], 1.0, [c["sel64"]])
        kb.memset("dve", c["ones_f"][:, :], 1.0, [c["ones_f"]])
        kb.memset("dve", c["epsb"][:, 0:1], EPS, [c["epsb"]])
        kb.dma("pool", c["ident"][:, :], c_ident, c["ident"])
        kb.dma("pool", c["maskT"][:, :], c_mask, c["maskT"])

        def stop(tag):
            if upto == tag:
                raise _Stop()

        try:
          stop("const")
          for l in range(depth):
              src = xT if l == 0 else xb
              srcB = [] if l == 0 else [B_xb]
              last = (l == depth - 1)
              S.barrier()
              with CleanStack() as es:
                  c["lnd"] = kb.sb(es, "lnd", [65, 512], F32)
                  c["rd"] = [kb.sb(es, "rd", [65, 512], F32) for _ in range(2)]
                  c["rdb"] = [kb.sb(es, "rdb", [128, 2, 512], BF16) for _ in range(1)]
                  kb.memset("pool", c["rdb"][0][:, :, :], 0.0, [c["rdb"][0]])
                  c["osb"] = [kb.sb(es, "osb", [64, 512], F32) for _ in range(2)]
                  c["pt"] = [kb.sb(es, "pt", [128, 512], BF16) for _ in range(5)]
                  c["sel"] = kb.sb(es, "sel", [4, 12 * 128], BF16)
                  c["zeros4"] = kb.sb(es, "zeros4", [4, 512], F32)
                  kb.memset("dve", c["zeros4"][:, :], 0.0, [c["zeros4"]])
                  kb.dma("pool", c["sel"][:, :], c_sel, c["sel"])
                  w1 = kb.sb(es, "w1", [128, 8, W1C], BF16)
                  par = kb.sb(es, "par", [128, NPAR], F32)
                  for k in range(8):
                      kb.dma("pool", w1[:, k, :], wa1[l, k * 128:(k + 1) * 128, :], w1)
                  kb.dma("sp", par[:, :], params[l], par)
                  kaug = [kb.sb(es, f"kaug{h}", [99, S_], BF16) for h in range(4)]
                  qaug = [kb.sb(es, f"qaug{h}", [99, 512], BF16) for h in range(4)]
                  vfox = kb.sb(es, "vfox", [128, NT, 4, 72], BF16)
                  qsw = [kb.sb(es, f"qsw{p}", [128, 512], BF16) for p in range(4)]
                  kdup = [[kb.sb(es, f"kdup{g}{ab}", [128, 1024], BF16) for ab in range(2)] for g in range(2)]
                  for g in range(2):
                      kb.memset("pool", kdup[g][0][64:128, :], 0.0, [kdup[g][0]])
                      kb.memset("pool", kdup[g][1][0:64, :], 0.0, [kdup[g][1]])
                  vsw = kb.sb(es, "vsw", [128, 8, 2, 72], BF16)
                  xblk = kb.sb(es, "xblk", [128, 8, 512], F32)
                  hT = [kb.sb(es, f"hT{k}", [128, 512], BF16) for k in range(8)]
                  grpA = kb.sb(es, "grpA", [64, 8, 512], F32)
                  grpB = grpA
                  mixt = [kb.sb(es, "mixt", [64, 12, 512], BF16) for _ in range(1)]
                  btab = kb.sb(es, "btab", [128, 2048], BF16)
                  kb.dma("sp", xblk[:, 0:4, :], c_biasg.rearrange("p (a b) -> p a b", a=4), xblk)
                  kb.dma("sp", xblk[:, 4:8, :], c_m01.rearrange("p (a b) -> p a b", a=4), xblk)
                  for q_ in range(4):
                      kb.ts("dve", xblk[:, q_, :], xblk[:, q_, :], 8.0, 30000.0, ALU.mult, ALU.add, [xblk], [xblk])
                      kb.tt("dve", xblk[:, q_, :], xblk[:, q_, :], xblk[:, 4 + q_, :], ALU.mult, [xblk], [xblk])
                      kb.ts("dve", btab[:, q_ * 512:(q_ + 1) * 512], xblk[:, q_, :], -30000.0, None, ALU.add, None, [xblk], [btab])
                  pmt = [kb.sb(es, "pmt", [128, 512], BF16) for _ in range(4)]
                  negfb = kb.sb(es, "negfb", [4, 1], F32)
                  e1 = kb.sb(es, "e1", [4, 512], F32)
                  nlf = e1
                  Gt = [kb.sb(es, "G", [4, 512], F32) for _ in range(2)]
                  r1 = kb.sb(es, "r1", [4, 512], F32)
                  r2 = r1
                  g3 = kb.sb(es, "g3", [4, 3, 512], BF16)

                  for h in range(4):
                      kb.memset("pool", kaug[h][64:99, :], 0.0, [kaug[h]])
                      kb.memset("pool", kaug[h][96:99, :], -8.0, [kaug[h]])
                      kb.memset("pool", qaug[h][64:99, :], 0.0, [qaug[h]])
                      kb.memset("pool", qaug[h][64:67, :], 8.0, [qaug[h]])
                  kb.memset("pool", vfox[:, :, :, :], 1.0, [vfox])
                  kb.memset("pool", vsw[:, :, :, :], 1.0, [vsw])
                  kb.ts("dve", negfb[0:4, 0:1], par[0:4, 51:52], -1.0, None, ALU.mult, None, [par], [negfb])
                  kb.act(c["esink"][:, :], par[:, 52:60], AF.Exp, [par], [c["esink"]])
                  skb = kb.sb(es, "skb", [128, 1024], BF16)
                  e64 = kb.sb(es, "e64", [128, 72], BF16)
                  kb.memset("pool", skb[:, :], 0.0, [skb])
                  kb.memset("pool", e64[:, :], 0.0, [e64])
                  kb.memset("pool", e64[0:1, 64:65], 1.0, [e64])
                  kb.memset("pool", e64[32:33, 64:65], 1.0, [e64])
                  for half, skf in enumerate((c["lnv"], c["rstd"][0])):
                      for hh in range(4):
                          h = half * 4 + hh
                          for p0 in (0, 32):
                              kb.ts("dve", skf[p0:p0 + 1, hh * 128:(hh + 1) * 128], c["ones_f"][p0:p0 + 1, 0:128],
                                    c["esink"][p0:p0 + 1, h:h + 1], None, ALU.mult, None, [c["ones_f"], c["esink"]], [skf])
                      hs = slice(half * 512, half * 512 + 512)
                      kb.cp("dve", skb[0:1, hs], skf[0:1, :], [skf], [skb])
                      kb.cp("dve", skb[32:33, hs], skf[32:33, :], [skf], [skb])
                      kb.tt("dve", skf[32:33, :], skf[32:33, :], skb[32:33, hs], ALU.subtract, [skf, skb], [skf])
                      kb.cp("dve", skb[32:33, hs], skf[32:33, :], [skf], [skb])
                  stop("a1load")

                  acc = kb.sb(es, "acc", [128, 512], F32)
                  tmpq = kb.sb(es, "tmpq", [128, 512], BF16)

                  def pn_acc():
                      for k in range(8):
                          kb.tt("pool", tmpq[:, :], xblk[:, k, :], xblk[:, k, :], ALU.mult, [xblk], [tmpq])
                          if k == 0:
                              kb.cp("pool", acc[:, :], tmpq[:, :], [tmpq], [acc])
                          else:
                              kb.tt("pool", acc[:, :], acc[:, :], tmpq[:, :], ALU.add, [acc, tmpq], [acc])

                  def pn_fin():
                      ps = c["ps_stat"]
                      kb.mm(ps[:, :], c["ones_f"][:, 0:128], acc[:, :], True, True, [c["ones_f"], acc], [ps])
                      lnv = c["lnv"]
                      rstd = kb.nxt("rstd", c["rstd"])
                      kb.act(lnv[:, :], ps[:, :], AF.Ln, [ps, c["epsb"]], [lnv], scale=1.0 / D, bias=c["epsb"][:, 0:1])
                      kb.act(rstd[:, :], lnv[:, :], AF.Exp, [lnv], [rstd], scale=-0.5)
                      for k in range(8):
                          kb.stt(hT[k][:, :], xblk[:, k, :], par[:, k:k + 1], rstd[:, :], ALU.mult, ALU.mult,
                                 [xblk, par, rstd], [hT[k]])

                  def pn_next(nxt_blk):
                      pn_fin()
                      if nxt_blk + 1 < nblk:
                          kb.load_x(src, (nxt_blk + 1) * 512, 512, xblk)
                          pn_acc()

                  kb.load_x(src, 0, 512, xblk)
                  pn_acc()
                  pn_next(0)
                  for i in range(nblk):
                      c0 = i * 512
                      ps = kb.proj_fm(w1, FF, 4, hT)
                      kb.act(e1[0:4, :], ps[0:4, :], AF.Exp, [ps, negfb], [e1], scale=-1.0, bias=negfb[0:4, 0:1])
                      kb.act(nlf[0:4, :], e1[0:4, :], AF.Ln, [e1, c["ones_f"]], [nlf], bias=c["ones_f"][0:4, 0:1])
                      G = Gt[i % 2]
                      Gp = Gt[(i + 1) % 2]
                      init = 0.0 if i == 0 else Gp[0:4, 511:512]
                      S.op("dve", (lambda G=G, init=init, nlf=nlf, z4=c["zeros4"]: lambda e: e.tensor_tensor_scan(
                          out=G[0:4, :], data0=z4[0:4, :], data1=nlf[0:4, :], initial=init,
                          op0=ALU.add, op1=ALU.add))(), [c["zeros4"], nlf, Gp], [G])
                      kb.cp("dve", g3[0:4, 0, :], G[0:4, :], [G], [g3])
                      kb.tt("dve", r1[0:4, :], G[0:4, :], g3[0:4, 0, :], ALU.subtract, [G, g3], [r1])
                      kb.cp("dve", g3[0:4, 1, :], r1[0:4, :], [r1], [g3])
                      kb.tt("dve", r2[0:4, :], r1[0:4, :], g3[0:4, 1, :], ALU.subtract, [r1, g3], [r2])
                      kb.cp("dve", g3[0:4, 2, :], r2[0:4, :], [r2], [g3])
                      stop("a1gate")
                      for p in range(4):
                          ps = kb.proj_fm(w1, QS + p * 128, 128, hT)
                          kb.evac(qsw[p][:, :], ps[:, :], [ps], [qsw[p]])
                      stop("p1")
                      rb = (i % 2) * 512
                      for g in range(2):
                          ps = kb.proj_fm(w1, KD + g * 128, 128, hT)
                          kb.cp("dve", kdup[g][0][0:64, rb:rb + 512], ps[0:64, :], [ps], [kdup[g][0]])
                          kb.cp("dve", kdup[g][1][64:128, rb:rb + 512], ps[64:128, :], [ps], [kdup[g][1]])
                      stop("p2")
                      for t4 in range(4):
                          ps = kb.nxt("pp", c["ps_proj"])
                          for k in range(8):
                              kb.mm(ps[:, 0:384], hT[k][:, t4 * 128:(t4 + 1) * 128], w1[:, k, TV:TV + 384], k == 0, k == 7,
                                    [hT[k], w1], [ps])
                          stop("p2a")
                          n = 4 * i + t4
                          for g in range(2):
                              kb.cp("act", vsw[:, n % 8, g, 0:64], ps[:, g * 64:(g + 1) * 64], [ps], [vsw])
                          stop("p2b")
                          for h in range(4):
                              kb.cp("dve", vfox[:, n, h, 0:64], ps[:, 128 + h * 64:128 + (h + 1) * 64], [ps], [vfox])
                          stop("p2c")
                          if t4 == 1:
                              stop("p2d")
                      stop("p3")
                      for h in range(4):
                          ps = kb.proj_fm(w1, FQ + h * 64, 64, hT)
                          kb.evac(qaug[h][0:64, :], ps[0:64, :], [ps], [qaug[h]])
                          ps = kb.proj_fm(w1, FK + h * 64, 64, hT)
                          kb.evac(kaug[h][0:64, c0:c0 + 512], ps[0:64, :], [ps], [kaug[h]])
                          stop("p4")
                          ps = kb.nxt("pp", c["ps_proj"])
                          for part in range(3):
                              o = (h * 3 + part) * 128
                              kb.mm(ps[0:99, :], c["sel"][0:4, o:o + 99], g3[0:4, part, :], part == 0, part == 2,
                                    [c["sel"], g3], [ps])
                          stop("p5")
                          kb.cp("dve", kaug[h][64:67, c0:c0 + 512], ps[64:67, :], [ps], [kaug[h]])
                          kb.cp("dve", qaug[h][96:99, :], ps[96:99, :], [ps], [qaug[h]])
                      stop("a1proj")
                      its = [(g, qt) for g in range(2) for qt in range(4)]
                      pend = {}

                      def s1(k):
                          g, qt = its[k]
                          n = 4 * i + qt
                          cur = rb + qt * 128
                          prv = (cur - 128) % 1024
                          pcs = ([0] if n > 0 else []) + [1]
                          pms = []
                          for pc in pcs:
                              kc = prv if pc == 0 else cur
                              ps_s = kb.nxt("pss", c["ps_s"])
                              eo = (g * 2 + pc) * 512
                              kb.mm(ps_s[:, :], c["ident"][:, :], btab[:, eo:eo + 512], True, False, [c["ident"], btab], [ps_s])
                              for hl in range(4):
                                  h = 4 * g + hl
                                  kd = kdup[g][h % 2]
                                  kb.mm(ps_s[:, hl * 128:(hl + 1) * 128], kd[:, kc:kc + 128],
                                        qsw[h // 2][:, qt * 128:(qt + 1) * 128], False, hl == 3,
                                        [kd, qsw[h // 2]], [ps_s])
                              pm = kb.nxt("pmt", pmt)
                              kb.act(pm[:, :], ps_s[:, :], AF.Exp, [ps_s], [pm], scale=0.125)
                              pms.append((pm, (n - 1 + pc) % 8))
                          pend[k] = pms

                      def s2a(k):
                          g, qt = its[k]
                          pms = pend.pop(k)
                          ps_o = kb.nxt("po", c["ps_o"])
                          for hl in range(4):
                              for idx, (pm, slot) in enumerate(pms):
                                  kb.mm(ps_o[0:65, hl * 128:(hl + 1) * 128], vsw[:, slot, g, 0:65],
                                        pm[:, hl * 128:(hl + 1) * 128], idx == 0, False,
                                        [vsw, pm], [ps_o])
                              so = g * 512 + hl * 128
                              kb.mm(ps_o[0:65, hl * 128:(hl + 1) * 128], e64[:, 0:65], skb[:, so:so + 128], False, True,
                                    [e64, skb], [ps_o])
                          rd = kb.finish_head(ps_o, None, 512)
                          pend[("o", k)] = (ps_o, rd)

                      def s2b(k):
                          g, qt = its[k]
                          ps_o, rd = pend.pop(("o", k))
                          osb, ps_bc = kb.finish_norm(ps_o, rd, 512, cpeng="act")
                          for hl in range(4):
                              h = 4 * g + hl
                              kb.tt("dve", grpA[0:64, h, qt * 128:(qt + 1) * 128], osb[0:64, hl * 128:(hl + 1) * 128],
                                    ps_bc[0:64, hl * 128:(hl + 1) * 128], ALU.mult, [osb, ps_bc], [grpA])

                      for k in range(len(its) + 3):
                          if k < len(its):
                              s1(k)
                          if 3 <= k:
                              s2b(k - 3)
                          if 1 <= k <= len(its):
                              s2a(k - 1)
                      mx = mixt[0]
                      rstd = kb.rms([grpA[0:64, h, :] for h in range(8)], [grpA] * 8, 64, 512, 512)
                      for h in range(8):
                          kb.stt(mx[0:64, h, :], grpA[0:64, h, :], par[0:64, 32 + h:33 + h], rstd[0:64, :],
                                 ALU.mult, ALU.mult, [grpA, par, rstd], [mx])
                      stop("a1swa")
                      hk = None
                      if i + 1 < nblk:
                          hk = {4 + 2 * (4 * i + 4): (lambda i=i: pn_next(i + 1))}
                      kb.attn_family(i, kaug, qaug, vfox, 99, 0.125, grpB, hooks=hk)
                      rstd = kb.rms([grpB[0:64, h, :] for h in range(4)], [grpB] * 4, 64, 256, 512)
                      for h in range(4):
                          kb.stt(mx[0:64, 8 + h, :], grpB[0:64, h, :], par[0:64, 40 + h:41 + h], rstd[0:64, :],
                                 ALU.mult, ALU.mult, [grpB, par, rstd], [mx])
                      kb.dma("sp", mixab.rearrange("(t p) s -> p t s", p=64)[:, :, c0:c0 + 512], mx[0:64, :, :], B_mix, [mx])

              stop("a1")
              S.barrier()
              with CleanStack() as es:
                  c["lnd"] = kb.sb(es, "lnd", [65, 512], F32)
                  c["rd"] = [kb.sb(es, "rd", [65, 512], F32) for _ in range(1)]
                  c["rdb"] = [kb.sb(es, "rdb", [128, 2, 512], BF16) for _ in range(1)]
                  kb.memset("pool", c["rdb"][0][:, :, :], 0.0, [c["rdb"][0]])
                  c["osb"] = [kb.sb(es, "osb", [64, 512], F32) for _ in range(2)]
                  c["pt"] = [kb.sb(es, "pt", [128, 512], BF16) for _ in range(5)]
                  w2 = kb.sb(es, "w2", [128, 8, W2C], BF16)
                  wq = kb.sb(es, "wq", [128, 2, 768], BF16)
                  wkv = kb.sb(es, "wkv", [128, 512], BF16)
                  wo = kb.sb(es, "wo", [128, 6, D], BF16)
                  woc = kb.sb(es, "woc", [64, 4, D], BF16)
                  par = kb.sb(es, "par", [128, NPAR], F32)
                  for k in range(8):
                      kb.dma("pool", w2[:, k, :], wa2[l, k * 128:(k + 1) * 128, :], w2)
                  for k in range(2):
                      kb.dma("pool", wq[:, k, :], wuq[l, k * 128:(k + 1) * 128, :], wq)
                  kb.dma("pool", wkv[:, :], wukv[l], wkv)
                  for r in range(6):
                      kb.dma("pool", wo[:, r, :], wout[l, r * 128:(r + 1) * 128, :], wo)
                  for r in range(4):
                      kb.dma("pool", woc[0:64, r, :], wout[l, 768 + r * 64:768 + (r + 1) * 64, :], woc)
                  kb.dma("sp", par[:, :], params[l], par)
                  kmla = [kb.sb(es, f"kmla{h}", [96, S_], BF16) for h in range(4)]
                  qmla = [kb.sb(es, f"qmla{h}", [96, 512], BF16) for h in range(4)]
                  vmla = kb.sb(es, "vmla", [128, NT, 4, 72], BF16)
                  xblk = kb.sb(es, "xblk", [128, 8, 512], F32)
                  hT = [kb.sb(es, f"hT{k}", [128, 512], BF16) for k in range(8)]
                  cq = [kb.sb(es, f"cq{k}", [128, 512], F32) for k in range(2)]
                  cqn = [kb.sb(es, f"cqn{k}", [128, 512], BF16) for k in range(2)]
                  ckv = kb.sb(es, "ckv", [128, 512], F32)
                  ckvn = kb.sb(es, "ckvn", [128, 512], BF16)
                  rope = kb.sb(es, "rope", [128, 2, 512], F32)
                  tm1 = [kb.sb(es, "tm1", [96, 512], F32) for _ in range(1)]
                  tm2 = [kb.sb(es, "tm2", [96, 512], F32) for _ in range(1)]
                  grpC = kb.sb(es, "grpC", [64, 4, 512], F32)
                  mixc = [kb.sb(es, f"mixc{h}", [64, 512], BF16) for h in range(4)]
                  mab = kb.sb(es, "mab", [128, 6, 512], BF16)
                  yT = [kb.sb(es, f"yT{m}", [128, 512], F32) for m in range(8)]
                  kb.memset("pool", vmla[:, :, :, :], 1.0, [vmla])
                  sc_mla = 96.0 ** -0.5

                  xblks2 = [xblk, kb.sb(es, "xblk2", [128, 8, 512], F32)]
                  def a2_tail(i):
                      c0 = i * 512
                      xb_ = xblks2[i % 2]
                      rstd = kb.rms([yT[m][:, :] for m in range(8)], yT, 128, D, 512)
                      for m in range(8):
                          kb.stt(yT[m][:, :], yT[m][:, :], par[:, 8 + m:9 + m], rstd[:, :], ALU.mult, ALU.mult,
                                 [yT[m], par, rstd], [yT[m]])
                          kb.tt("pool", yT[m][:, :], yT[m][:, :], xb_[:, m, :], ALU.add, [yT[m], xb_], [yT[m]])
                          kb.dma("sp", xa[m * 128:(m + 1) * 128, c0:c0 + 512], yT[m][:, :], B_xa, [yT[m]])

                  acc2 = kb.sb(es, "acc2", [128, 512], F32)
                  tmpq2 = kb.sb(es, "tmpq2", [128, 512], BF16)

                  def pn_acc2(xb_):
                      for k in range(8):
                          kb.tt("pool", tmpq2[:, :], xb_[:, k, :], xb_[:, k, :], ALU.mult, [xb_], [tmpq2])
                          if k == 0:
                              kb.cp("pool", acc2[:, :], tmpq2[:, :], [tmpq2], [acc2])
                          else:
                              kb.tt("pool", acc2[:, :], acc2[:, :], tmpq2[:, :], ALU.add, [acc2, tmpq2], [acc2])

                  def pn_fin2(xb_):
                      ps = c["ps_stat"]
                      kb.mm(ps[:, :], c["ones_f"][:, 0:128], acc2[:, :], True, True, [c["ones_f"], acc2], [ps])
                      lnv = c["lnv"]
                      rstd = kb.nxt("rstd", c["rstd"])
                      kb.act(lnv[:, :], ps[:, :], AF.Ln, [ps, c["epsb"]], [lnv], scale=1.0 / D, bias=c["epsb"][:, 0:1])
                      kb.act(rstd[:, :], lnv[:, :], AF.Exp, [lnv], [rstd], scale=-0.5)
                      for k in range(8):
                          kb.stt(hT[k][:, :], xb_[:, k, :], par[:, k:k + 1], rstd[:, :], ALU.mult, ALU.mult,
                                 [xb_, par, rstd], [hT[k]])

                  kb.load_x(src, 0, 512, xblks2[0])
                  pn_acc2(xblks2[0])
                  pn_fin2(xblks2[0])
                  for i in range(nblk):
                      c0 = i * 512
                      xblk = xblks2[i % 2]
                      kb.dma("sp", rope[64:96, 0, :], c_rope[0, 64:96, c0:c0 + 512], rope)
                      kb.dma("sp", rope[64:96, 1, :], c_rope[1, 64:96, c0:c0 + 512], rope)
                      kb.dma("sp", mab[:, :, :], mixab.rearrange("(t p) s -> p t s", p=128)[:, :, c0:c0 + 512], mab, [B_mix])
                      for k in range(2):
                          ps = kb.proj_fm(w2, k * 128, 128, hT)
                          kb.evac(cq[k][:, :], ps[:, :], [ps], [cq[k]])
                      ps = kb.proj_fm(w2, 256, 128, hT)
                      kb.evac(ckv[:, :], ps[:, :], [ps], [ckv])
                      rstd = kb.rms([cq[0][:, :], cq[1][:, :]], cq, 128, 256, 512)
                      for k in range(2):
                          kb.stt(cqn[k][:, :], cq[k][:, :], par[:, 48 + k:49 + k], rstd[:, :], ALU.mult, ALU.mult,
                                 [cq[k], par, rstd], [cqn[k]])
                      rstd = kb.rms([ckv[:, :]], [ckv], 128, 128, 512)
                      kb.stt(ckvn[:, :], ckv[:, :], par[:, 50:51], rstd[:, :], ALU.mult, ALU.mult, [ckv, par, rstd], [ckvn])
                      psa = kb.proj_fm(w2, 384, 96, hT)
                      psb = kb.proj_fm(w2, 480, 96, hT)
                      t1 = kb.nxt("tm1", tm1)
                      t2 = kb.nxt("tm2", tm2)
                      kb.tt("dve", t1[64:96, :], psa[64:96, :], rope[64:96, 0, :], ALU.mult, [psa, rope], [t1])
                      kb.tt("dve", t2[64:96, :], psb[64:96, :], rope[64:96, 1, :], ALU.mult, [psb, rope], [t2])
                      for h in range(4):
                          kb.tt("pool" if h % 2 else "dve", kmla[h][64:96, c0:c0 + 512], t1[64:96, :], t2[64:96, :], ALU.add,
                                [t1, t2], [kmla[h]])
                      for h in range(4):
                          ps = kb.nxt("pp", c["ps_proj"])
                          kb.mm(ps[0:64, :], wkv[:, h * 64:(h + 1) * 64], ckvn[:, :], True, True, [wkv, ckvn], [ps])
                          kb.evac(kmla[h][0:64, c0:c0 + 512], ps[0:64, :], [ps], [kmla[h]])
                      for t4 in range(4):
                          ps = kb.nxt("pp", c["ps_proj"])
                          kb.mm(ps[:, 0:256], ckvn[:, t4 * 128:(t4 + 1) * 128], wkv[:, 256:512], True, True, [ckvn, wkv], [ps])
                          for h in range(4):
                              kb.cp("dve" if h % 2 else "act", vmla[:, 4 * i + t4, h, 0:64], ps[:, h * 64:(h + 1) * 64], [ps], [vmla])
                      if i > 0:
                          a2_tail(i - 1)
                      if i + 1 < nblk:
                          kb.load_x(src, c0 + 512, 512, xblks2[(i + 1) % 2])
                          pn_acc2(xblks2[(i + 1) % 2])
                      for h in range(4):
                          psa = kb.nxt("pp", c["ps_proj"])
                          for k in range(2):
                              kb.mm(psa[0:96, :], wq[:, k, h * 96:(h + 1) * 96], cqn[k][:, :], k == 0, k == 1, [wq, cqn[k]], [psa])
                          psb = kb.nxt("pp", c["ps_proj"])
                          for k in range(2):
                              kb.mm(psb[0:96, :], wq[:, k, 384 + h * 96:384 + (h + 1) * 96], cqn[k][:, :], k == 0, k == 1,
                                    [wq, cqn[k]], [psb])
                          kb.cp("dve", qmla[h][0:64, :], psa[0:64, :], [psa], [qmla[h]])
                          t1 = kb.nxt("tm1", tm1)
                          t2 = kb.nxt("tm2", tm2)
                          kb.tt("dve", t1[64:96, :], psa[64:96, :], rope[64:96, 0, :], ALU.mult, [psa, rope], [t1])
                          kb.tt("dve", t2[64:96, :], psb[64:96, :], rope[64:96, 1, :], ALU.mult, [psb, rope], [t2])
                          kb.tt("pool", qmla[h][64:96, :], t1[64:96, :], t2[64:96, :], ALU.add, [t1, t2], [qmla[h]])
                      kb.attn_family(i, kmla, qmla, vmla, 96, sc_mla, grpC)
                      rstd = kb.rms([grpC[0:64, h, :] for h in range(4)], [grpC] * 4, 64, 256, 512)
                      for h in range(4):
                          kb.stt(mixc[h][0:64, :], grpC[0:64, h, :], par[0:64, 44 + h:45 + h], rstd[0:64, :],
                                 ALU.mult, ALU.mult, [grpC, par, rstd], [mixc[h]])
                      for m in range(8):
                          ps = kb.nxt("pp", c["ps_proj"])
                          for r in range(6):
                              kb.mm(ps[:, :], wo[:, r, m * 128:(m + 1) * 128], mab[:, r, :], r == 0, False, [wo, mab], [ps])
                          for r in range(4):
                              kb.mm(ps[:, :], woc[0:64, r, m * 128:(m + 1) * 128], mixc[r][0:64, :], False, r == 3,
                                    [woc, mixc[r]], [ps])
                          kb.cp("act", yT[m][:, :], ps[:, :], [ps], [yT[m]])
                      if i + 1 < nblk:
                          pn_fin2(xblks2[(i + 1) % 2])
                  a2_tail(nblk - 1)

              stop("a2")
              S.barrier()
              with CleanStack() as es:
                  wugrp = [kb.sb(es, f"wu{q4}", [128, 8, 1408], BF16) for q4 in range(4)]
                  wd = kb.sb(es, "wd", [128, 22, D], BF16)
                  par = kb.sb(es, "par", [128, NPAR], F32)
                  kb.dma("sp", par[:, :], params[l], par)
                  for q4 in (0, 2, 1, 3):
                      for k in range(8):
                          kb.dma("pool", wugrp[q4][:, k, :],
                                 wup[l, k * 128:(k + 1) * 128, q4 * 1408:(q4 + 1) * 1408], wugrp[q4])
                  for r in range(22):
                      kb.dma("pool", wd[:, r, :], wdn[l, r * 128:(r + 1) * 128, :], wd)
                  xblks = [kb.sb(es, "xblk", [128, 8, FB], F32) for _ in range(2)]
                  hTs = [[kb.sb(es, f"hT{k}", [128, FB + 2], BF16) for k in range(8)] for _ in range(2)]
                  aT = [kb.sb(es, f"aT{r}", [128, FB], BF16) for r in range(22)]
                  yT = [kb.sb(es, f"yT{m}", [128, FB], F32) for m in range(8)]
                  yb = [kb.sb(es, "yb", [128, FB], F32) for _ in range(6)]
                  gl = [kb.sb(es, "gl", [128, FB], F32) for _ in range(2)]
                  ps_up = [pst[j] for j in (1, 2, 3, 4, 5)]
                  ps_dn = [pst[6], pst[7]]
                  for k in range(8):
                      kb.memset("pool", hTs[0][k][:, 0:2], 0.0, [hTs[0][k]])
                  dst_ap, dst_b = (outT, B_out) if last else (xb, B_xb)
                  nfb = S_ // FB

                  sqF = [kb.sb(es, "sqF", [128, FB], BF16) for _ in range(8)]

                  def stageA_ld(i):
                      kb.load_x(xa, i * FB, FB, xblks[i % 2])

                  def stageA_sq(i):
                      xb_, hT_ = xblks[i % 2], hTs[i % 2]
                      if i > 0:
                          hp = hTs[(i - 1) % 2]
                          for k in range(8):
                              kb.cp("pool", hT_[k][:, 0:2], hp[k][:, FB:FB + 2], [hp[k]], [hT_[k]])
                      kb.rms_sq([xb_[:, k, 0:FB] for k in range(8)], [xb_] * 8, 128, FB, sqF)

                  def stageA_fin(i):
                      xb_, hT_ = xblks[i % 2], hTs[i % 2]
                      rstd = kb.rms_fin(8, 128, D, FB, sqF)
                      for k in range(8):
                          kb.stt(hT_[k][:, 2:2 + FB], xb_[:, k, 0:FB], par[:, 16 + k:17 + k], rstd[:, 0:FB],
                                 ALU.mult, ALU.mult, [xb_, par, rstd], [hT_[k]])

                  def stageB(i, hook=None):
                      hT_ = hTs[i % 2]
                      for r in range(22):
                          if hook is not None:
                              hook(r)
                          ys = []
                          for which in range(2):
                              ch = r + 22 * which
                              ps = kb.nxt("pup", ps_up)
                              for k in range(8):
                                  wg = wugrp[ch // 11]
                                  lc = (ch % 11) * 128
                                  kb.mm(ps[:, 0:FB + 2], wg[:, k, lc:lc + 128], hT_[k][:, 0:FB + 2], k == 0, k == 7,
                                        [wg, hT_[k]], [ps])
                              y = kb.nxt("yb", yb)
                              pc = 60 + ch * 4
                              kb.act(y[:, :], ps[:, 2:FB + 2], AF.Identity, [ps, par], [y], scale=par[:, pc + 2:pc + 3],
                                     bias=par[:, pc + 3:pc + 4])
                              kb.stt(y[:, :], ps[:, 1:FB + 1], par[:, pc + 1:pc + 2], y[:, :], ALU.mult, ALU.add, [ps, par, y], [y])
                              kb.stt(y[:, :], ps[:, 0:FB], par[:, pc:pc + 1], y[:, :], ALU.mult, ALU.add, [ps, par, y], [y])
                              ys.append(y)
                          g_ = kb.nxt("gl", gl)
                          kb.act(g_[:, :], ys[0][:, :], AF.Gelu_apprx_tanh, [ys[0]], [g_])
                          kb.tt("pool", aT[r][:, :], g_[:, :], ys[1][:, :], ALU.mult, [g_, ys[1]], [aT[r]])

                  def stageC(i):
                      c0 = i * FB
                      xb_ = xblks[i % 2]
                      for m in range(8):
                          ps = kb.nxt("pdn", ps_dn)
                          for r in range(22):
                              kb.mm(ps[:, 0:FB], wd[:, r, m * 128:(m + 1) * 128], aT[r][:, :], r == 0, r == 21, [wd, aT[r]], [ps])
                          kb.cp("act", yT[m][:, :], ps[:, 0:FB], [ps], [yT[m]])

                  def stageCt_sq(i):
                      kb.rms_sq([yT[m][:, :] for m in range(8)], yT, 128, FB, sqF)

                  def stageCt_fin(i):
                      c0 = i * FB
                      xb_ = xblks[i % 2]
                      rstd = kb.rms_fin(8, 128, D, FB, sqF)
                      for m in range(8):
                          kb.stt(yT[m][:, :], yT[m][:, :], par[:, 24 + m:25 + m], rstd[:, 0:FB], ALU.mult, ALU.mult,
                                 [yT[m], par, rstd], [yT[m]])
                          kb.tt("pool", yT[m][:, :], yT[m][:, :], xb_[:, m, :], ALU.add, [yT[m], xb_], [yT[m]])
                          kb.dma("sp", dst_ap[m * 128:(m + 1) * 128, c0:c0 + FB], yT[m][:, :], dst_b, [yT[m]])

                  stageA_ld(0)
                  stageA_sq(0)
                  stageA_fin(0)
                  for i in range(nfb):
                      def hook(r, i=i):
                          if r == 1 and i > 0:
                              stageCt_sq(i - 1)
                          if r == 5:
                              if i > 0:
                                  stageCt_fin(i - 1)
                              if i + 1 < nfb:
                                  stageA_ld(i + 1)
                          if r == 10 and i + 1 < nfb:
                              stageA_sq(i + 1)
                          if r == 15 and i + 1 < nfb:
                              stageA_fin(i + 1)
                      stageB(i, hook)
                      stageC(i)
                  stageCt_sq(nfb - 1)
                  stageCt_fin(nfb - 1)

        except _Stop:
            S.barrier()
            with CleanStack() as es:
                tcp = kb.sb(es, "tcp", [128, 8, 512], F32)
                kb.dma("sp", tcp[:, :, :], xT.rearrange("(k p) s -> p k s", p=128)[:, :, 0:512], tcp)
                kb.dma("sp", outT.rearrange("(k p) s -> p k s", p=128)[:, :, 0:512], tcp[:, :, :], B_out, [tcp])
        S.wait_all("sp", [B_out])
        S.barrier()
        keys = ["pe", "act", "dve", "pool"] + list(S.dma_bufs.keys())
        sems = {k: top.enter_context(nc.semaphore(f"s{j}")) for j, k in enumerate(keys)}
        kb.nsems = len(keys)
        with nc.Block() as block:
            @block.tensor
            def _(e):
                S.emit_one(e, "pe", sems)

            @block.scalar
            def _(e):
                S.emit_one(e, "act", sems)

            @block.vector
            def _(e):
                S.emit_one(e, "dve", sems)

            @block.gpsimd
            def _(e):
                S.emit_one(e, "pool", sems)

            @block.sync
            def _(e):
                S.emit_one(e, "sp", sems)
    return nc, kb


def _t5_bucket(dist):
    d = np.maximum(dist, 0)
    lr = np.log(np.maximum(d, 1).astype(np.float32) / np.float32(16)) / np.float32(math.log(128 / 16))
    large = 16 + (lr * 16).astype(np.int32)
    large = np.minimum(large, 31)
    return np.where(d < 16, d, large)


def host_prep(inp, S_, depth):
    f = np.float32
    L = depth
    w_in = np.asarray(inp["w_in"], f)[:L]
    A0, Fo, M0 = 0, 768, 1540
    cols1 = np.concatenate([
        np.arange(0, 512),
        np.arange(512, 576), np.arange(512, 576), np.arange(576, 640), np.arange(576, 640),
        np.arange(640, 768), Fo + np.arange(512, 768),
        Fo + np.arange(0, 256), Fo + np.arange(256, 512), Fo + np.arange(768, 772)])
    assert cols1.size == W1C
    wa1 = np.ascontiguousarray(w_in[:, :, cols1])
    kr = M0 + 384 + np.arange(32)
    krp = np.concatenate([kr[16:], kr[:16]])
    dmy = M0 + np.arange(64)
    cols2 = np.concatenate([M0 + np.arange(0, 384), dmy, kr, dmy, krp])
    assert cols2.size == W2C
    wa2 = np.ascontiguousarray(w_in[:, :, cols2])
    w_uq = np.asarray(inp["w_uq"], f)[:L]
    cq_ = []
    for h in range(4):
        cq_.append(np.arange(h * 96, (h + 1) * 96))
    for h in range(4):
        rp = h * 96 + 64 + np.arange(32)
        cq_.append(np.concatenate([np.arange(h * 96, h * 96 + 64), rp[16:], rp[:16]]))
    wuq = np.ascontiguousarray(w_uq[:, :, np.concatenate(cq_)])
    w_ukv = np.asarray(inp["w_ukv"], f)[:L]
    ck = np.concatenate([h * 128 + np.arange(64) for h in range(4)] + [h * 128 + 64 + np.arange(64) for h in range(4)])
    wukv = np.ascontiguousarray(w_ukv[:, :, ck])
    params = np.zeros((L, 128, NPAR), f)

    def pc(v):
        return v.reshape(L, -1, 128).transpose(0, 2, 1)

    params[:, :, 0:8] = pc(np.asarray(inp["attn_pre_norm"], f)[:L])
    params[:, :, 8:16] = pc(np.asarray(inp["attn_post_norm"], f)[:L])
    params[:, :, 16:24] = pc(np.asarray(inp["ffn_pre_norm"], f)[:L])
    params[:, :, 24:32] = pc(np.asarray(inp["ffn_post_norm"], f)[:L])
    gn = np.asarray(inp["group_norm"], f)[:L].reshape(L, 16, 64).transpose(0, 2, 1)
    params[:, 0:64, 32:48] = gn
    params[:, 64:128, 32:48] = gn
    params[:, :, 48:50] = pc(np.asarray(inp["q_latent_norm"], f)[:L])
    params[:, :, 50:51] = pc(np.asarray(inp["kv_latent_norm"], f)[:L])
    params[:, 0:4, 51] = np.asarray(inp["forget_bias"], f)[:L]
    params[:, :, 52:60] = np.asarray(inp["swa_sinks"], f)[:L][:, None, :]
    cw = np.asarray(inp["conv_w"], f)[:L]
    cb = np.asarray(inp["conv_b"], f)[:L]
    cv = np.concatenate([cw, cb[:, None, :]], axis=1)
    cv = cv.reshape(L, 4, 44, 128).transpose(0, 3, 2, 1)
    params[:, :, 60:] = cv.reshape(L, 128, 176)
    ident = np.eye(128, dtype=f)
    s_ = np.arange(128)[:, None]
    t_ = np.arange(128)[None, :]
    maskT = np.where(s_ <= t_, 0.0, -30000.0).astype(f)
    sel = np.zeros((4, 12, 128), f)
    for h in range(4):
        for p in range(3):
            sel[h, h * 3 + p, 64 + p] = 1.0
            sel[h, h * 3 + p, 96 + p] = 1.0
    pos = np.arange(S_, dtype=f)
    inv = (np.float32(10000.0) ** (-(np.arange(16, dtype=f) * np.float32(2.0) / np.float32(32)))).astype(f)
    ang = (pos[:, None] * inv[None, :]).astype(f)
    cs, sn = np.cos(ang).astype(f).T, np.sin(ang).astype(f).T
    rope = np.zeros((2, 128, S_), f)
    rope[0, 64:80] = cs
    rope[0, 80:96] = cs
    rope[1, 64:80] = -sn
    rope[1, 80:96] = sn
    rel = np.asarray(inp["rel_bias"], f)
    biasg = np.zeros((128, 2, 2, 4, 128), f)
    m01 = np.zeros((128, 2, 2, 4, 128), f)
    q_ = np.arange(128)[None, :]
    for pcx in range(2):
        dist = (q_ + 128 - s_) if pcx == 0 else (q_ - s_)
        ok = (dist >= 0) & (dist < 128)
        bk = _t5_bucket(np.clip(dist, 0, 127))
        for g in range(2):
            for hl in range(4):
                biasg[:, g, pcx, hl, :] = rel[bk, 4 * g + hl]
                m01[:, g, pcx, hl, :] = ok.astype(f)
    common = dict(wa1=wa1, wa2=wa2, wuq=wuq, wukv=wukv,
                  wout=np.ascontiguousarray(np.asarray(inp["w_out"], f)[:L]),
                  wup=np.ascontiguousarray(np.asarray(inp["w_up"], f)[:L]),
                  wdn=np.ascontiguousarray(np.asarray(inp["w_down"], f)[:L]),
                  params=params, c_ident=ident, c_mask=maskT, c_sel=sel.reshape(4, 12 * 128),
                  c_rope=rope, c_biasg=biasg.reshape(128, 2048), c_m01=m01.reshape(128, 2048))
    return common


_CACHE = {}


def run(inp, nblk=8, depth=DEPTH, ncores=8):
    S_ = nblk * 512
    key = (nblk, depth)
    if key not in _CACHE:
        _CACHE[key] = build_program(nblk, depth)[0]
    nc = _CACHE[key]
    common = host_prep(inp, S_, depth)
    x = np.asarray(inp["x"], np.float32)
    in_maps = []
    for b in range(ncores):
        m = dict(common)
        m["xT"] = np.ascontiguousarray(x[b, :S_, :].T)
        in_maps.append(m)
    res = run_bass_kernel_spmd(nc, in_maps, core_ids=list(range(ncores)))
    out = np.stack([np.ascontiguousarray(r["outT"].T) for r in res.results], axis=0)
    return out.astype(np.float32)


def kernel(**inputs):
    return run(inputs, nblk=SEQ // 512, depth=DEPTH, ncores=8)
```

```python
import math
import numpy as np
from contextlib import ExitStack
import concourse.bass as bass
import concourse.mybir as mybir
from concourse.bass_utils import run_bass_kernel_spmd

F32 = mybir.dt.float32
BF16 = mybir.dt.bfloat16
AF = mybir.ActivationFunctionType
ALU = mybir.AluOpType

D = 1024
DEPTH = 4
SEQ = 4096
NPAR = 60 + 176
EPS = 1e-6
W1C = 1668
QS, KD, TV, FQ, FK, FF = 0, 512, 768, 1152, 1408, 1664
W2C = 576
DFF = 2816
FB = 256

ENGS = ["pe", "act", "dve", "pool", "sp"]


class Buf:
    __slots__ = ("name", "w", "r", "dcnt", "key", "excl")

    def __init__(self, name):
        self.name = name
        self.excl = False
        self.w = None
        self.r = {}
        self.dcnt = 0
        self.key = ("dma", name)


class Tl:
    def __init__(self, t, name):
        self.t = t
        self.b = Buf(name)

    def __getitem__(self, idx):
        return self.t[idx]


def _b(x):
    return x.b if isinstance(x, Tl) else x


class Sched:
    def __init__(self):
        self.ops = {e: [] for e in ENGS}
        self.cnt = {e: 0 for e in ENGS}
        self.seen = {e: {} for e in ENGS}
        self.dma_bufs = {}

    def _deps(self, eng, reads, writes, own_key=None):
        need = {}
        for b in reads:
            if b.w is not None:
                k, v = b.w
                if need.get(k, 0) < v:
                    need[k] = v
            if b.excl:
                for k, v in b.r.items():
                    if k != eng and need.get(k, 0) < v:
                        need[k] = v
        for b in writes:
            if b.w is not None:
                k, v = b.w
                if need.get(k, 0) < v:
                    need[k] = v
            for k, v in b.r.items():
                if need.get(k, 0) < v:
                    need[k] = v
        waits = []
        seen = self.seen[eng]
        for k, v in need.items():
            if k == eng and eng == "pe":
                continue
            if own_key is not None and k == own_key:
                continue
            if k in self.dma_bufs:
                v = self.dma_bufs[k].dcnt
            if seen.get(k, 0) >= v:
                continue
            seen[k] = v
            waits.append((k, v))
        return waits

    def op(self, eng, fn, reads=(), writes=()):
        reads = [_b(x) for x in reads]
        writes = [_b(x) for x in writes]
        waits = self._deps(eng, reads, writes)
        self.cnt[eng] += 1
        v = self.cnt[eng]
        for b in reads:
            if b.r.get(eng, 0) < v:
                b.r[eng] = v
        for b in writes:
            b.w = (eng, v)
            b.r = {}
        self.ops[eng].append((waits, fn, eng, 1))

    def dma(self, q, fn, dst, reads=()):
        dst = _b(dst)
        reads = [_b(x) for x in reads]
        key = dst.key
        self.dma_bufs[key] = dst
        waits = self._deps(q, reads, [dst], own_key=key)
        dst.dcnt += 16
        for b in reads:
            if b.r.get(key, 0) < dst.dcnt:
                b.r[key] = dst.dcnt
        dst.w = (key, dst.dcnt)
        dst.r = {}
        self.ops[q].append((waits, fn, key, 16))

    def barrier(self):
        tot = {e: self.cnt[e] for e in ("pe", "act", "dve", "pool")}
        for k, b in self.dma_bufs.items():
            tot[k] = b.dcnt
        for e in ENGS:
            waits = []
            for k, v in tot.items():
                if k == e or v == 0:
                    continue
                if self.seen[e].get(k, 0) >= v:
                    continue
                self.seen[e][k] = v
                waits.append((k, v))
            if waits:
                self.ops[e].append((waits, None, None, 0))

    def wait_all(self, eng, bufs):
        waits = self._deps(eng, [_b(x) for x in bufs], [])
        self.ops[eng].append((waits, None, None, 0))

    def emit_one(self, eng, name, sems):
        for waits, fn, key, inc in self.ops[name]:
            for k, v in waits:
                eng.wait_ge(sems[k], v)
            if fn is not None:
                fn(eng).then_inc(sems[key], inc)


class KB:
    def __init__(self, nc, nblk, depth):
        self.nc = nc
        self.S = Sched()
        self.uid = 0
        self.nblk = nblk
        self.depth = depth
        self.rot = {}

    def sb(self, es, name, shape, dt):
        self.uid += 1
        nm = f"{name}_{self.uid}"
        return Tl(es.enter_context(self.nc.sbuf_tensor(nm, shape, dt)), nm)

    def nxt(self, key, lst):
        i = self.rot.get(key, 0)
        self.rot[key] = i + 1
        return lst[i % len(lst)]

    def mm(self, out, lhsT, rhs, start, stop, R, W):
        self.S.op("pe", lambda e: e.matmul(out, lhsT=lhsT, rhs=rhs, start=start, stop=stop), R, W)

    def act(self, out, in_, func, R, W, scale=None, bias=None):
        kw = {}
        if scale is not None:
            kw["scale"] = scale
        if bias is not None:
            kw["bias"] = bias
        self.S.op("act", lambda e: e.activation(out=out, in_=in_, func=func, **kw), R, W)

    def tt(self, eng, out, in0, in1, op, R, W):
        self.S.op(eng, lambda e: e.tensor_tensor(out=out, in0=in0, in1=in1, op=op), R, W)

    def ts(self, eng, out, in0, s1, s2, op0, op1, R, W):
        if op1 is None:
            self.S.op(eng, lambda e: e.tensor_scalar(out=out, in0=in0, scalar1=s1, scalar2=None, op0=op0), R, W)
        else:
            self.S.op(eng, lambda e: e.tensor_scalar(out=out, in0=in0, scalar1=s1, scalar2=s2, op0=op0, op1=op1), R, W)

    def stt(self, out, in0, scalar, in1, op0, op1, R, W):
        self.S.op("dve", lambda e: e.scalar_tensor_tensor(out=out, in0=in0, scalar=scalar, in1=in1, op0=op0, op1=op1), R, W)

    def cp(self, eng, out, in_, R, W):
        if eng == "act":
            self.S.op("act", lambda e: e.activation(out=out, in_=in_, func=AF.Copy), R, W)
        else:
            self.S.op(eng, lambda e: e.tensor_copy(out=out, in_=in_), R, W)

    def memset(self, eng, ap, val, W):
        self.S.op(eng, lambda e: e.memset(ap, val), (), W)

    def dma(self, q, out, in_, dst, R=()):
        self.S.dma(q, lambda e: e.dma_start(out=out, in_=in_), dst, R)

    def rms_sq(self, srcs, bufs, P, n, sqt, engs=("pool", "dve", "act", "pool", "dve", "pool", "dve", "act")):
        for j, (ap, b) in enumerate(zip(srcs, bufs)):
            sq = sqt[j]
            en = engs[j % len(engs)]
            if en == "act":
                self.act(sq[0:P, 0:n], ap, AF.Square, [b], [sq])
            else:
                self.tt(en, sq[0:P, 0:n], ap, ap, ALU.mult, [b], [sq])

    def rms_fin(self, nsrc, P, Dn, n, sqt):
        c = self.c
        ps = c["ps_stat"]
        for j in range(nsrc):
            sq = sqt[j]
            self.mm(ps[:, 0:n], c["ones_bf"][0:P, 0:128], sq[0:P, 0:n], j == 0, j == nsrc - 1,
                    [sq, c["ones_bf"]], [ps])
        lnv = c["lnv"]
        rstd = self.nxt("rstd", c["rstd"])
        self.act(lnv[:, 0:n], ps[:, 0:n], AF.Ln, [ps, c["epsb"]], [lnv], scale=1.0 / Dn, bias=c["epsb"][:, 0:1])
        self.act(rstd[:, 0:n], lnv[:, 0:n], AF.Exp, [lnv], [rstd], scale=-0.5)
        return rstd

    def rms(self, srcs, bufs, P, Dn, n, engs=("pool", "dve", "act", "pool", "dve", "pool", "dve", "act")):
        c = self.c
        ps = c["ps_stat"]
        for j, (ap, b) in enumerate(zip(srcs, bufs)):
            sq = self.nxt("sq", c["sq"])
            en = engs[j % len(engs)]
            if en == "act":
                self.act(sq[0:P, 0:n], ap, AF.Square, [b], [sq])
            else:
                self.tt(en, sq[0:P, 0:n], ap, ap, ALU.mult, [b], [sq])
            self.mm(ps[:, 0:n], c["ones_bf"][0:P, 0:128], sq[0:P, 0:n], j == 0, j == len(srcs) - 1,
                    [sq, c["ones_bf"]], [ps])
        lnv = c["lnv"]
        rstd = self.nxt("rstd", c["rstd"])
        self.act(lnv[:, 0:n], ps[:, 0:n], AF.Ln, [ps, c["epsb"]], [lnv], scale=1.0 / Dn, bias=c["epsb"][:, 0:1])
        self.act(rstd[:, 0:n], lnv[:, 0:n], AF.Exp, [lnv], [rstd], scale=-0.5)
        return rstd

    def proj_fm(self, w, col0, M, hT, n=512):
        ps = self.nxt("pp", self.c["ps_proj"])
        for k in range(8):
            self.mm(ps[0:M, 0:n], w[:, k, col0:col0 + M], hT[k][:, 0:n], k == 0, k == 7, [w, hT[k]], [ps])
        return ps

    def load_x(self, src, c0, n, xblk, srcB=()):
        self.dma("sp", xblk[:, :, 0:n], src.rearrange("(k p) s -> p k s", p=128)[:, :, c0:c0 + n], xblk, srcB)

    def prenorm(self, src, c0, n, xblk, hT, par, gcol, off=0, load=True):
        if load:
            self.load_x(src, c0, n, xblk)
        rstd = self.rms([xblk[:, k, 0:n] for k in range(8)], [xblk] * 8, 128, D, n)
        for k in range(8):
            self.stt(hT[k][:, off:off + n], xblk[:, k, 0:n], par[:, gcol + k:gcol + k + 1], rstd[:, 0:n],
                     ALU.mult, ALU.mult, [xblk, par, rstd], [hT[k]])

    def attn_family(self, i, kcs, qts, vc, KD_, scale, grp, hooks=None):
        c = self.c
        ntile = 4 * i + 4
        tasks = [(h, j) for h in range(4) for j in range(ntile)]
        T = len(tasks)
        L = 4
        st = {}
        pso = {}
        deferred = []
        for t in range(T + L + 5):
            if L <= t < T + L:
                h, j = tasks[t - L]
                ps_s, q0 = st.pop(t - L)
                if j == 0:
                    pso[h] = self.nxt("po", c["ps_o"])
                ps_o = pso[h]
                pt = self.nxt("pt", c["pt"])
                self.act(pt[:, q0:512], ps_s[:, q0:512], AF.Exp, [ps_s], [pt], scale=scale)
                self.mm(ps_o[0:65, q0:512], vc[:, j, h, 0:65], pt[:, q0:512], j == 0, j == ntile - 1, [vc, pt], [ps_o])
                if j == ntile - 1:
                    rd = self.finish_head(ps_o, None, 512)

                    def fin(ps_o=ps_o, rd=rd, h=h):
                        osb, ps_bc = self.finish_norm(ps_o, rd, 512)
                        self.tt("dve", grp[0:64, h, :], osb[0:64, :], ps_bc[0:64, :], ALU.mult, [osb, ps_bc], [grp])
                    deferred.append((t + 3, fin))
            if t < T:
                h, j = tasks[t]
                d = j - 4 * i
                q0 = 128 * d if d > 0 else 0
                ps_s = self.nxt("pss", c["ps_s"])
                self.mm(ps_s[:, q0:512], kcs[h][0:KD_, j * 128:(j + 1) * 128], qts[h][0:KD_, q0:512], True, d < 0,
                        [kcs[h], qts[h]], [ps_s])
                if d >= 0:
                    self.mm(ps_s[:, q0:q0 + 128], c["ident"][:, :], c["maskT"][:, :], False, True,
                            [c["ident"], c["maskT"]], [ps_s])
                st[t] = (ps_s, q0)
            while deferred and deferred[0][0] <= t:
                deferred.pop(0)[1]()
            if hooks and t in hooks:
                hooks[t]()
        assert not deferred and not st

    def finish_head(self, ps_o, sink_ap, n, col0=0, rd=None):
        c = self.c
        lnd = c["lnd"]
        if rd is None:
            rd = self.nxt("rd", c["rd"])
        sl = slice(col0, col0 + n)
        if sink_ap is None:
            self.act(lnd[64:65, sl], ps_o[64:65, sl], AF.Ln, [ps_o], [lnd])
        else:
            self.act(lnd[64:65, sl], ps_o[64:65, sl], AF.Ln, [ps_o, c["esink"]], [lnd], bias=sink_ap)
        self.act(rd[64:65, sl], lnd[64:65, sl], AF.Exp, [lnd], [rd], scale=-1.0)
        return rd

    def finish_norm(self, ps_o, rd, n, cpeng="dve"):
        c = self.c
        ps_bc = c["ps_bc"]
        rb = self.nxt("rdb", c["rdb"])
        self.cp("dve", rb[64:65, 0, 0:n], rd[64:65, 0:n], [rd], [rb])
        self.stt(rb[64:65, 1, 0:n], rb[64:65, 0, 0:n], -1.0, rd[64:65, 0:n], ALU.mult, ALU.add, [rb, rd], [rb])
        self.mm(ps_bc[0:64, 0:n], c["sel64"][:, 0:64], rb[:, 0, 0:n], True, False, [c["sel64"], rb], [ps_bc])
        self.mm(ps_bc[0:64, 0:n], c["sel64"][:, 0:64], rb[:, 1, 0:n], False, True, [c["sel64"], rb], [ps_bc])
        osb = self.nxt("osb", c["osb"])
        self.cp(cpeng, osb[0:64, 0:n], ps_o[0:64, 0:n], [ps_o], [osb])
        return osb, ps_bc

    def evac(self, out, in_, R, W):
        k = self.rot.get("evac", 0)
        self.rot["evac"] = k + 1
        self.cp("act" if k % 4 == 3 else "dve", out, in_, R, W)


class _Stop(Exception):
    pass


class CleanStack(ExitStack):
    def __exit__(self, *exc):
        super().__exit__(None, None, None)
        return False


def build_program(nblk=8, depth=DEPTH, upto=None):
    S_ = nblk * 512
    NT = S_ // 128
    nc = bass.Bass("TRN2", target_bir_lowering=False)

    def din(name, shape, dt=F32):
        return nc.dram_tensor(name, shape, dt, kind="ExternalInput").ap()

    xT = din("xT", [D, S_])
    wa1 = din("wa1", [depth, D, W1C])
    wa2 = din("wa2", [depth, D, W2C])
    wuq = din("wuq", [depth, 256, 768])
    wukv = din("wukv", [depth, 128, 512])
    wout = din("wout", [depth, D, D])
    wup = din("wup", [depth, D, 2 * DFF])
    wdn = din("wdn", [depth, DFF, D])
    params = din("params", [depth, 128, NPAR])
    c_ident = din("c_ident", [128, 128])
    c_mask = din("c_mask", [128, 128])
    c_sel = din("c_sel", [4, 12 * 128])
    c_rope = din("c_rope", [2, 128, S_])
    c_biasg = din("c_biasg", [128, 2048])
    c_m01 = din("c_m01", [128, 2048])
    outT = nc.dram_tensor("outT", [D, S_], F32, kind="ExternalOutput").ap()
    xa = nc.dram_tensor("xa", [D, S_], F32, kind="Internal").ap()
    xb = nc.dram_tensor("xb", [D, S_], F32, kind="Internal").ap()
    mixab = nc.dram_tensor("mixab", [768, S_], BF16, kind="Internal").ap()
    B_xa, B_xb, B_mix, B_out = Buf("xa"), Buf("xb"), Buf("mixab"), Buf("outT")

    kb = KB(nc, nblk, depth)
    S = kb.S
    with CleanStack() as top:
        c = kb.c = {}
        c["ones_bf"] = kb.sb(top, "ones_bf", [128, 128], BF16)
        c["ones_f"] = kb.sb(top, "ones_f", [128, 128], F32)
        c["ident"] = kb.sb(top, "ident", [128, 128], BF16)
        c["maskT"] = kb.sb(top, "maskT", [128, 128], BF16)
        c["epsb"] = kb.sb(top, "epsb", [128, 2], F32)
        c["sq"] = [kb.sb(top, "sq", [128, 512], BF16) for _ in range(3)]
        c["lnv"] = kb.sb(top, "lnv", [128, 512], F32)
        c["rstd"] = [kb.sb(top, "rstd", [128, 512], F32) for _ in range(2)]
        c["esink"] = kb.sb(top, "esink", [128, 8], F32)
        pst = [Tl(top.enter_context(nc.psum_tensor(f"psum{j}", [128, 512], F32)), f"psum{j}") for j in range(8)]
        for p_ in pst:
            p_.b.excl = True
        c["ps_stat"] = pst[0]
        c["ps_proj"] = [pst[1], pst[2], pst[3], pst[4]]
        c["ps_s"] = [pst[3], pst[4], pst[1], pst[2]]
        c["ps_o"] = [pst[5], pst[6]]
        c["ps_bc"] = pst[7]

        kb.memset("dve", c["ones_bf"][:, :], 1.0, [c["ones_bf"]])
        c["sel64"] = kb.sb(top, "sel64", [128, 64], BF16)
        kb.memset("dve", c["sel64"][:, :], 0.0, [c["sel64"]])
        kb.memset("dve", c["sel64"][64:65, :], 1.0, [c["sel64"]])
        kb.memset("dve", c["ones_f"][:, :], 1.0, [c["ones_f"]])
        kb.memset("dve", c["epsb"][:, 0:1], EPS, [c["epsb"]])
        kb.dma("pool", c["ident"][:, :], c_ident, c["ident"])
        kb.dma("pool", c["maskT"][:, :], c_mask, c["maskT"])

        def stop(tag):
            if upto == tag:
                raise _Stop()

        try:
          stop("const")
          for l in range(depth):
              src = xT if l == 0 else xb
              srcB = [] if l == 0 else [B_xb]
              last = (l == depth - 1)
              S.barrier()
              with CleanStack() as es:
                  c["lnd"] = kb.sb(es, "lnd", [65, 512], F32)
                  c["rd"] = [kb.sb(es, "rd", [65, 512], F32) for _ in range(2)]
                  c["rdb"] = [kb.sb(es, "rdb", [128, 2, 512], BF16) for _ in range(1)]
                  kb.memset("pool", c["rdb"][0][:, :, :], 0.0, [c["rdb"][0]])
                  c["osb"] = [kb.sb(es, "osb", [64, 512], F32) for _ in range(2)]
                  c["pt"] = [kb.sb(es, "pt", [128, 512], BF16) for _ in range(5)]
                  c["sel"] = kb.sb(es, "sel", [4, 12 * 128], BF16)
                  c["zeros4"] = kb.sb(es, "zeros4", [4, 512], F32)
                  kb.memset("dve", c["zeros4"][:, :], 0.0, [c["zeros4"]])
                  kb.dma("pool", c["sel"][:, :], c_sel, c["sel"])
                  w1 = kb.sb(es, "w1", [128, 8, W1C], BF16)
                  par = kb.sb(es, "par", [128, NPAR], F32)
                  for k in range(8):
                      kb.dma("pool", w1[:, k, :], wa1[l, k * 128:(k + 1) * 128, :], w1)
                  kb.dma("sp", par[:, :], params[l], par)
                  kaug = [kb.sb(es, f"kaug{h}", [99, S_], BF16) for h in range(4)]
                  qaug = [kb.sb(es, f"qaug{h}", [99, 512], BF16) for h in range(4)]
                  vfox = kb.sb(es, "vfox", [128, NT, 4, 72], BF16)
                  qsw = [kb.sb(es, f"qsw{p}", [128, 512], BF16) for p in range(4)]
                  kdup = [[kb.sb(es, f"kdup{g}{ab}", [128, 1024], BF16) for ab in range(2)] for g in range(2)]
                  for g in range(2):
                      kb.memset("pool", kdup[g][0][64:128, :], 0.0, [kdup[g][0]])
                      kb.memset("pool", kdup[g][1][0:64, :], 0.0, [kdup[g][1]])
                  vsw = kb.sb(es, "vsw", [128, 8, 2, 72], BF16)
                  xblk = kb.sb(es, "xblk", [128, 8, 512], F32)
                  hT = [kb.sb(es, f"hT{k}", [128, 512], BF16) for k in range(8)]
                  grpA = kb.sb(es, "grpA", [64, 8, 512], F32)
                  grpB = grpA
                  mixt = [kb.sb(es, "mixt", [64, 12, 512], BF16) for _ in range(1)]
                  btab = kb.sb(es, "btab", [128, 2048], BF16)
                  kb.dma("sp", xblk[:, 0:4, :], c_biasg.rearrange("p (a b) -> p a b", a=4), xblk)
                  kb.dma("sp", xblk[:, 4:8, :], c_m01.rearrange("p (a b) -> p a b", a=4), xblk)
                  for q_ in range(4):
                      kb.ts("dve", xblk[:, q_, :], xblk[:, q_, :], 8.0, 30000.0, ALU.mult, ALU.add, [xblk], [xblk])
                      kb.tt("dve", xblk[:, q_, :], xblk[:, q_, :], xblk[:, 4 + q_, :], ALU.mult, [xblk], [xblk])
                      kb.ts("dve", btab[:, q_ * 512:(q_ + 1) * 512], xblk[:, q_, :], -30000.0, None, ALU.add, None, [xblk], [btab])
                  pmt = [kb.sb(es, "pmt", [128, 512], BF16) for _ in range(4)]
                  negfb = kb.sb(es, "negfb", [4, 1], F32)
                  e1 = kb.sb(es, "e1", [4, 512], F32)
                  nlf = e1
                  Gt = [kb.sb(es, "G", [4, 512], F32) for _ in range(2)]
                  r1 = kb.sb(es, "r1", [4, 512], F32)
                  r2 = r1
                  g3 = kb.sb(es, "g3", [4, 3, 512], BF16)

                  for h in range(4):
                      kb.memset("pool", kaug[h][64:99, :], 0.0, [kaug[h]])
                      kb.memset("pool", kaug[h][96:99, :], -8.0, [kaug[h]])
                      kb.memset("pool", qaug[h][64:99, :], 0.0, [qaug[h]])
                      kb.memset("pool", qaug[h][64:67, :], 8.0, [qaug[h]])
                  kb.memset("pool", vfox[:, :, :, :], 1.0, [vfox])
                  kb.memset("pool", vsw[:, :, :, :], 1.0, [vsw])
                  kb.ts("dve", negfb[0:4, 0:1], par[0:4, 51:52], -1.0, None, ALU.mult, None, [par], [negfb])
                  kb.act(c["esink"][:, :], par[:, 52:60], AF.Exp, [par], [c["esink"]])
                  skb = kb.sb(es, "skb", [128, 1024], BF16)
                  e64 = kb.sb(es, "e64", [128, 72], BF16)
                  kb.memset("pool", skb[:, :], 0.0, [skb])
                  kb.memset("pool", e64[:, :], 0.0, [e64])
                  kb.memset("pool", e64[0:1, 64:65], 1.0, [e64])
                  kb.memset("pool", e64[32:33, 64:65], 1.0, [e64])
                  for half, skf in enumerate((c["lnv"], c["rstd"][0])):
                      for hh in range(4):
                          h = half * 4 + hh
                          for p0 in (0, 32):
                              kb.ts("dve", skf[p0:p0 + 1, hh * 128:(hh + 1) * 128], c["ones_f"][p0:p0 + 1, 0:128],
                                    c["esink"][p0:p0 + 1, h:h + 1], None, ALU.mult, None, [c["ones_f"], c["esink"]], [skf])
                      hs = slice(half * 512, half * 512 + 512)
                      kb.cp("dve", skb[0:1, hs], skf[0:1, :], [skf], [skb])
                      kb.cp("dve", skb[32:33, hs], skf[32:33, :], [skf], [skb])
                      kb.tt("dve", skf[32:33, :], skf[32:33, :], skb[32:33, hs], ALU.subtract, [skf, skb], [skf])
                      kb.cp("dve", skb[32:33, hs], skf[32:33, :], [skf], [skb])
                  stop("a1load")

                  acc = kb.sb(es, "acc", [128, 512], F32)
                  tmpq = kb.sb(es, "tmpq", [128, 512], BF16)

                  def pn_acc():
                      for k in range(8):
                          kb.tt("pool", tmpq[:, :], xblk[:, k, :], xblk[:, k, :], ALU.mult, [xblk], [tmpq])
                          if k == 0:
                              kb.cp("pool", acc[:, :], tmpq[:, :], [tmpq], [acc])
                          else:
                              kb.tt("pool", acc[:, :], acc[:, :], tmpq[:, :], ALU.add, [acc, tmpq], [acc])

                  def pn_fin():
                      ps = c["ps_stat"]
                      kb.mm(ps[:, :], c["ones_f"][:, 0:128], acc[:, :], True, True, [c["ones_f"], acc], [ps])
                      lnv = c["lnv"]
                      rstd = kb.nxt("rstd", c["rstd"])
                      kb.act(lnv[:, :], ps[:, :], AF.Ln, [ps, c["epsb"]], [lnv], scale=1.0 / D, bias=c["epsb"][:, 0:1])
                      kb.act(rstd[:, :], lnv[:, :], AF.Exp, [lnv], [rstd], scale=-0.5)
                      for k in range(8):
                          kb.stt(hT[k][:, :], xblk[:, k, :], par[:, k:k + 1], rstd[:, :], ALU.mult, ALU.mult,
                                 [xblk, par, rstd], [hT[k]])

                  def pn_next(nxt_blk):
                      pn_fin()
                      if nxt_blk + 1 < nblk:
                          kb.load_x(src, (nxt_blk + 1) * 512, 512, xblk)
                          pn_acc()

                  kb.load_x(src, 0, 512, xblk)
                  pn_acc()
                  pn_next(0)
                  for i in range(nblk):
                      c0 = i * 512
                      ps = kb.proj_fm(w1, FF, 4, hT)
                      kb.act(e1[0:4, :], ps[0:4, :], AF.Exp, [ps, negfb], [e1], scale=-1.0, bias=negfb[0:4, 0:1])
                      kb.act(nlf[0:4, :], e1[0:4, :], AF.Ln, [e1, c["ones_f"]], [nlf], bias=c["ones_f"][0:4, 0:1])
                      G = Gt[i % 2]
                      Gp = Gt[(i + 1) % 2]
                      init = 0.0 if i == 0 else Gp[0:4, 511:512]
                      S.op("dve", (lambda G=G, init=init, nlf=nlf, z4=c["zeros4"]: lambda e: e.tensor_tensor_scan(
                          out=G[0:4, :], data0=z4[0:4, :], data1=nlf[0:4, :], initial=init,
                          op0=ALU.add, op1=ALU.add))(), [c["zeros4"], nlf, Gp], [G])
                      kb.cp("dve", g3[0:4, 0, :], G[0:4, :], [G], [g3])
                      kb.tt("dve", r1[0:4, :], G[0:4, :], g3[0:4, 0, :], ALU.subtract, [G, g3], [r1])
                      kb.cp("dve", g3[0:4, 1, :], r1[0:4, :], [r1], [g3])
                      kb.tt("dve", r2[0:4, :], r1[0:4, :], g3[0:4, 1, :], ALU.subtract, [r1, g3], [r2])
                      kb.cp("dve", g3[0:4, 2, :], r2[0:4, :], [r2], [g3])
                      stop("a1gate")
                      for p in range(4):
                          ps = kb.proj_fm(w1, QS + p * 128, 128, hT)
                          kb.evac(qsw[p][:, :], ps[:, :], [ps], [qsw[p]])
                      stop("p1")
                      rb = (i % 2) * 512
                      for g in range(2):
                          ps = kb.proj_fm(w1, KD + g * 128, 128, hT)
                          kb.cp("dve", kdup[g][0][0:64, rb:rb + 512], ps[0:64, :], [ps], [kdup[g][0]])
                          kb.cp("dve", kdup[g][1][64:128, rb:rb + 512], ps[64:128, :], [ps], [kdup[g][1]])
                      stop("p2")
                      for t4 in range(4):
                          ps = kb.nxt("pp", c["ps_proj"])
                          for k in range(8):
                              kb.mm(ps[:, 0:384], hT[k][:, t4 * 128:(t4 + 1) * 128], w1[:, k, TV:TV + 384], k == 0, k == 7,
                                    [hT[k], w1], [ps])
                          stop("p2a")
                          n = 4 * i + t4
                          for g in range(2):
                              kb.cp("act", vsw[:, n % 8, g, 0:64], ps[:, g * 64:(g + 1) * 64], [ps], [vsw])
                          stop("p2b")
                          for h in range(4):
                              kb.cp("dve", vfox[:, n, h, 0:64], ps[:, 128 + h * 64:128 + (h + 1) * 64], [ps], [vfox])
                          stop("p2c")
                          if t4 == 1:
                              stop("p2d")
                      stop("p3")
                      for h in range(4):
                          ps = kb.proj_fm(w1, FQ + h * 64, 64, hT)
                          kb.evac(qaug[h][0:64, :], ps[0:64, :], [ps], [qaug[h]])
                          ps = kb.proj_fm(w1, FK + h * 64, 64, hT)
                          kb.evac(kaug[h][0:64, c0:c0 + 512], ps[0:64, :], [ps], [kaug[h]])
                          stop("p4")
                          ps = kb.nxt("pp", c["ps_proj"])
                          for part in range(3):
                              o = (h * 3 + part) * 128
                              kb.mm(ps[0:99, :], c["sel"][0:4, o:o + 99], g3[0:4, part, :], part == 0, part == 2,
                                    [c["sel"], g3], [ps])
                          stop("p5")
                          kb.cp("dve", kaug[h][64:67, c0:c0 + 512], ps[64:67, :], [ps], [kaug[h]])
                          kb.cp("dve", qaug[h][96:99, :], ps[96:99, :], [ps], [qaug[h]])
                      stop("a1proj")
                      its = [(g, qt) for g in range(2) for qt in range(4)]
                      pend = {}

                      def s1(k):
                          g, qt = its[k]
                          n = 4 * i + qt
                          cur = rb + qt * 128
                          prv = (cur - 128) % 1024
                          pcs = ([0] if n > 0 else []) + [1]
                          pms = []
                          for pc in pcs:
                              kc = prv if pc == 0 else cur
                              ps_s = kb.nxt("pss", c["ps_s"])
                              eo = (g * 2 + pc) * 512
                              kb.mm(ps_s[:, :], c["ident"][:, :], btab[:, eo:eo + 512], True, False, [c["ident"], btab], [ps_s])
                              for hl in range(4):
                                  h = 4 * g + hl
                                  kd = kdup[g][h % 2]
                                  kb.mm(ps_s[:, hl * 128:(hl + 1) * 128], kd[:, kc:kc + 128],
                                        qsw[h // 2][:, qt * 128:(qt + 1) * 128], False, hl == 3,
                                        [kd, qsw[h // 2]], [ps_s])
                              pm = kb.nxt("pmt", pmt)
                              kb.act(pm[:, :], ps_s[:, :], AF.Exp, [ps_s], [pm], scale=0.125)
                              pms.append((pm, (n - 1 + pc) % 8))
                          pend[k] = pms

                      def s2a(k):
                          g, qt = its[k]
                          pms = pend.pop(k)
                          ps_o = kb.nxt("po", c["ps_o"])
                          for hl in range(4):
                              for idx, (pm, slot) in enumerate(pms):
                                  kb.mm(ps_o[0:65, hl * 128:(hl + 1) * 128], vsw[:, slot, g, 0:65],
                                        pm[:, hl * 128:(hl + 1) * 128], idx == 0, False,
                                        [vsw, pm], [ps_o])
                              so = g * 512 + hl * 128
                              kb.mm(ps_o[0:65, hl * 128:(hl + 1) * 128], e64[:, 0:65], skb[:, so:so + 128], False, True,
                                    [e64, skb], [ps_o])
                          rd = kb.finish_head(ps_o, None, 512)
                          pend[("o", k)] = (ps_o, rd)

                      def s2b(k):
                          g, qt = its[k]
                          ps_o, rd = pend.pop(("o", k))
                          osb, ps_bc = kb.finish_norm(ps_o, rd, 512, cpeng="act")
                          for hl in range(4):
                              h = 4 * g + hl
                              kb.tt("dve", grpA[0:64, h, qt * 128:(qt + 1) * 128], osb[0:64, hl * 128:(hl + 1) * 128],
                                    ps_bc[0:64, hl * 128:(hl + 1) * 128], ALU.mult, [osb, ps_bc], [grpA])

                      for k in range(len(its) + 3):
                          if k < len(its):
                              s1(k)
                          if 3 <= k:
                              s2b(k - 3)
                          if 1 <= k <= len(its):
                              s2a(k - 1)
                      mx = mixt[0]
                      rstd = kb.rms([grpA[0:64, h, :] for h in range(8)], [grpA] * 8, 64, 512, 512)
                      for h in range(8):
                          kb.stt(mx[0:64, h, :], grpA[0:64, h, :], par[0:64, 32 + h:33 + h], rstd[0:64, :],
                                 ALU.mult, ALU.mult, [grpA, par, rstd], [mx])
                      stop("a1swa")
                      hk = None
                      if i + 1 < nblk:
                          hk = {4 + 2 * (4 * i + 4): (lambda i=i: pn_next(i + 1))}
                      kb.attn_family(i, kaug, qaug, vfox, 99, 0.125, grpB, hooks=hk)
                      rstd = kb.rms([grpB[0:64, h, :] for h in range(4)], [grpB] * 4, 64, 256, 512)
                      for h in range(4):
                          kb.stt(mx[0:64, 8 + h, :], grpB[0:64, h, :], par[0:64, 40 + h:41 + h], rstd[0:64, :],
                                 ALU.mult, ALU.mult, [grpB, par, rstd], [mx])
                      kb.dma("sp", mixab.rearrange("(t p) s -> p t s", p=64)[:, :, c0:c0 + 512], mx[0:64, :, :], B_mix, [mx])

              stop("a1")
              S.barrier()
              with CleanStack() as es:
                  c["lnd"] = kb.sb(es, "lnd", [65, 512], F32)
                  c["rd"] = [kb.sb(es, "rd", [65, 512], F32) for _ in range(1)]
                  c["rdb"] = [kb.sb(es, "rdb", [128, 2, 512], BF16) for _ in range(1)]
                  kb.memset("pool", c["rdb"][0][:, :, :], 0.0, [c["rdb"][0]])
                  c["osb"] = [kb.sb(es, "osb", [64, 512], F32) for _ in range(2)]
                  c["pt"] = [kb.sb(es, "pt", [128, 512], BF16) for _ in range(5)]
                  w2 = kb.sb(es, "w2", [128, 8, W2C], BF16)
                  wq = kb.sb(es, "wq", [128, 2, 768], BF16)
                  wkv = kb.sb(es, "wkv", [128, 512], BF16)
                  wo = kb.sb(es, "wo", [128, 6, D], BF16)
                  woc = kb.sb(es, "woc", [64, 4, D], BF16)
                  par = kb.sb(es, "par", [128, NPAR], F32)
                  for k in range(8):
                      kb.dma("pool", w2[:, k, :], wa2[l, k * 128:(k + 1) * 128, :], w2)
                  for k in range(2):
                      kb.dma("pool", wq[:, k, :], wuq[l, k * 128:(k + 1) * 128, :], wq)
                  kb.dma("pool", wkv[:, :], wukv[l], wkv)
                  for r in range(6):
                      kb.dma("pool", wo[:, r, :], wout[l, r * 128:(r + 1) * 128, :], wo)
                  for r in range(4):
                      kb.dma("pool", woc[0:64, r, :], wout[l, 768 + r * 64:768 + (r + 1) * 64, :], woc)
                  kb.dma("sp", par[:, :], params[l], par)
                  kmla = [kb.sb(es, f"kmla{h}", [96, S_], BF16) for h in range(4)]
                  qmla = [kb.sb(es, f"qmla{h}", [96, 512], BF16) for h in range(4)]
                  vmla = kb.sb(es, "vmla", [128, NT, 4, 72], BF16)
                  xblk = kb.sb(es, "xblk", [128, 8, 512], F32)
                  hT = [kb.sb(es, f"hT{k}", [128, 512], BF16) for k in range(8)]
                  cq = [kb.sb(es, f"cq{k}", [128, 512], F32) for k in range(2)]
                  cqn = [kb.sb(es, f"cqn{k}", [128, 512], BF16) for k in range(2)]
                  ckv = kb.sb(es, "ckv", [128, 512], F32)
                  ckvn = kb.sb(es, "ckvn", [128, 512], BF16)
                  rope = kb.sb(es, "rope", [128, 2, 512], F32)
                  tm1 = [kb.sb(es, "tm1", [96, 512], F32) for _ in range(1)]
                  tm2 = [kb.sb(es, "tm2", [96, 512], F32) for _ in range(1)]
                  grpC = kb.sb(es, "grpC", [64, 4, 512], F32)
                  mixc = [kb.sb(es, f"mixc{h}", [64, 512], BF16) for h in range(4)]
                  mab = kb.sb(es, "mab", [128, 6, 512], BF16)
                  yT = [kb.sb(es, f"yT{m}", [128, 512], F32) for m in range(8)]
                  kb.memset("pool", vmla[:, :, :, :], 1.0, [vmla])
                  sc_mla = 96.0 ** -0.5

                  xblks2 = [xblk, kb.sb(es, "xblk2", [128, 8, 512], F32)]
                  def a2_tail(i):
                      c0 = i * 512
                      xb_ = xblks2[i % 2]
                      rstd = kb.rms([yT[m][:, :] for m in range(8)], yT, 128, D, 512)
                      for m in range(8):
                          kb.stt(yT[m][:, :], yT[m][:, :], par[:, 8 + m:9 + m], rstd[:, :], ALU.mult, ALU.mult,
                                 [yT[m], par, rstd], [yT[m]])
                          kb.tt("pool", yT[m][:, :], yT[m][:, :], xb_[:, m, :], ALU.add, [yT[m], xb_], [yT[m]])
                          kb.dma("sp", xa[m * 128:(m + 1) * 128, c0:c0 + 512], yT[m][:, :], B_xa, [yT[m]])

                  kb.load_x(src, 0, 512, xblks2[0])
                  kb.prenorm(src, 0, 512, xblks2[0], hT, par, 0, load=False)
                  for i in range(nblk):
                      c0 = i * 512
                      xblk = xblks2[i % 2]
                      kb.dma("sp", rope[64:96, 0, :], c_rope[0, 64:96, c0:c0 + 512], rope)
                      kb.dma("sp", rope[64:96, 1, :], c_rope[1, 64:96, c0:c0 + 512], rope)
                      kb.dma("sp", mab[:, :, :], mixab.rearrange("(t p) s -> p t s", p=128)[:, :, c0:c0 + 512], mab, [B_mix])
                      for k in range(2):
                          ps = kb.proj_fm(w2, k * 128, 128, hT)
                          kb.evac(cq[k][:, :], ps[:, :], [ps], [cq[k]])
                      ps = kb.proj_fm(w2, 256, 128, hT)
                      kb.evac(ckv[:, :], ps[:, :], [ps], [ckv])
                      rstd = kb.rms([cq[0][:, :], cq[1][:, :]], cq, 128, 256, 512)
                      for k in range(2):
                          kb.stt(cqn[k][:, :], cq[k][:, :], par[:, 48 + k:49 + k], rstd[:, :], ALU.mult, ALU.mult,
                                 [cq[k], par, rstd], [cqn[k]])
                      rstd = kb.rms([ckv[:, :]], [ckv], 128, 128, 512)
                      kb.stt(ckvn[:, :], ckv[:, :], par[:, 50:51], rstd[:, :], ALU.mult, ALU.mult, [ckv, par, rstd], [ckvn])
                      psa = kb.proj_fm(w2, 384, 96, hT)
                      psb = kb.proj_fm(w2, 480, 96, hT)
                      t1 = kb.nxt("tm1", tm1)
                      t2 = kb.nxt("tm2", tm2)
                      kb.tt("dve", t1[64:96, :], psa[64:96, :], rope[64:96, 0, :], ALU.mult, [psa, rope], [t1])
                      kb.tt("dve", t2[64:96, :], psb[64:96, :], rope[64:96, 1, :], ALU.mult, [psb, rope], [t2])
                      for h in range(4):
                          kb.tt("pool" if h % 2 else "dve", kmla[h][64:96, c0:c0 + 512], t1[64:96, :], t2[64:96, :], ALU.add,
                                [t1, t2], [kmla[h]])
                      for h in range(4):
                          ps = kb.nxt("pp", c["ps_proj"])
                          kb.mm(ps[0:64, :], wkv[:, h * 64:(h + 1) * 64], ckvn[:, :], True, True, [wkv, ckvn], [ps])
                          kb.evac(kmla[h][0:64, c0:c0 + 512], ps[0:64, :], [ps], [kmla[h]])
                      for t4 in range(4):
                          ps = kb.nxt("pp", c["ps_proj"])
                          kb.mm(ps[:, 0:256], ckvn[:, t4 * 128:(t4 + 1) * 128], wkv[:, 256:512], True, True, [ckvn, wkv], [ps])
                          for h in range(4):
                              kb.cp("dve" if h % 2 else "act", vmla[:, 4 * i + t4, h, 0:64], ps[:, h * 64:(h + 1) * 64], [ps], [vmla])
                      if i > 0:
                          a2_tail(i - 1)
                      if i + 1 < nblk:
                          kb.load_x(src, c0 + 512, 512, xblks2[(i + 1) % 2])
                      for h in range(4):
                          psa = kb.nxt("pp", c["ps_proj"])
                          for k in range(2):
                              kb.mm(psa[0:96, :], wq[:, k, h * 96:(h + 1) * 96], cqn[k][:, :], k == 0, k == 1, [wq, cqn[k]], [psa])
                          psb = kb.nxt("pp", c["ps_proj"])
                          for k in range(2):
                              kb.mm(psb[0:96, :], wq[:, k, 384 + h * 96:384 + (h + 1) * 96], cqn[k][:, :], k == 0, k == 1,
                                    [wq, cqn[k]], [psb])
                          kb.cp("dve", qmla[h][0:64, :], psa[0:64, :], [psa], [qmla[h]])
                          t1 = kb.nxt("tm1", tm1)
                          t2 = kb.nxt("tm2", tm2)
                          kb.tt("dve", t1[64:96, :], psa[64:96, :], rope[64:96, 0, :], ALU.mult, [psa, rope], [t1])
                          kb.tt("dve", t2[64:96, :], psb[64:96, :], rope[64:96, 1, :], ALU.mult, [psb, rope], [t2])
                          kb.tt("pool", qmla[h][64:96, :], t1[64:96, :], t2[64:96, :], ALU.add, [t1, t2], [qmla[h]])
                      kb.attn_family(i, kmla, qmla, vmla, 96, sc_mla, grpC)
                      rstd = kb.rms([grpC[0:64, h, :] for h in range(4)], [grpC] * 4, 64, 256, 512)
                      for h in range(4):
                          kb.stt(mixc[h][0:64, :], grpC[0:64, h, :], par[0:64, 44 + h:45 + h], rstd[0:64, :],
                                 ALU.mult, ALU.mult, [grpC, par, rstd], [mixc[h]])
                      for m in range(8):
                          ps = kb.nxt("pp", c["ps_proj"])
                          for r in range(6):
                              kb.mm(ps[:, :], wo[:, r, m * 128:(m + 1) * 128], mab[:, r, :], r == 0, False, [wo, mab], [ps])
                          for r in range(4):
                              kb.mm(ps[:, :], woc[0:64, r, m * 128:(m + 1) * 128], mixc[r][0:64, :], False, r == 3,
                                    [woc, mixc[r]], [ps])
                          kb.cp("act", yT[m][:, :], ps[:, :], [ps], [yT[m]])
                      if i + 1 < nblk:
                          kb.prenorm(src, c0 + 512, 512, xblks2[(i + 1) % 2], hT, par, 0, load=False)
                  a2_tail(nblk - 1)

              stop("a2")
              S.barrier()
              with CleanStack() as es:
                  wugrp = [kb.sb(es, f"wu{q4}", [128, 8, 1408], BF16) for q4 in range(4)]
                  wd = kb.sb(es, "wd", [128, 22, D], BF16)
                  par = kb.sb(es, "par", [128, NPAR], F32)
                  kb.dma("sp", par[:, :], params[l], par)
                  for q4 in (0, 2, 1, 3):
                      for k in range(8):
                          kb.dma("pool", wugrp[q4][:, k, :],
                                 wup[l, k * 128:(k + 1) * 128, q4 * 1408:(q4 + 1) * 1408], wugrp[q4])
                  for r in range(22):
                      kb.dma("pool", wd[:, r, :], wdn[l, r * 128:(r + 1) * 128, :], wd)
                  xblks = [kb.sb(es, "xblk", [128, 8, FB], F32) for _ in range(2)]
                  hTs = [[kb.sb(es, f"hT{k}", [128, FB + 2], BF16) for k in range(8)] for _ in range(2)]
                  aT = [kb.sb(es, f"aT{r}", [128, FB], BF16) for r in range(22)]
                  yT = [kb.sb(es, f"yT{m}", [128, FB], F32) for m in range(8)]
                  yb = [kb.sb(es, "yb", [128, FB], F32) for _ in range(6)]
                  gl = [kb.sb(es, "gl", [128, FB], F32) for _ in range(2)]
                  ps_up = [pst[j] for j in (1, 2, 3, 4, 5)]
                  ps_dn = [pst[6], pst[7]]
                  for k in range(8):
                      kb.memset("pool", hTs[0][k][:, 0:2], 0.0, [hTs[0][k]])
                  dst_ap, dst_b = (outT, B_out) if last else (xb, B_xb)
                  nfb = S_ // FB

                  sqF = [kb.sb(es, "sqF", [128, FB], BF16) for _ in range(8)]

                  def stageA_ld(i):
                      kb.load_x(xa, i * FB, FB, xblks[i % 2])

                  def stageA_sq(i):
                      xb_, hT_ = xblks[i % 2], hTs[i % 2]
                      if i > 0:
                          hp = hTs[(i - 1) % 2]
                          for k in range(8):
                              kb.cp("pool", hT_[k][:, 0:2], hp[k][:, FB:FB + 2], [hp[k]], [hT_[k]])
                      kb.rms_sq([xb_[:, k, 0:FB] for k in range(8)], [xb_] * 8, 128, FB, sqF)

                  def stageA_fin(i):
                      xb_, hT_ = xblks[i % 2], hTs[i % 2]
                      rstd = kb.rms_fin(8, 128, D, FB, sqF)
                      for k in range(8):
                          kb.stt(hT_[k][:, 2:2 + FB], xb_[:, k, 0:FB], par[:, 16 + k:17 + k], rstd[:, 0:FB],
                                 ALU.mult, ALU.mult, [xb_, par, rstd], [hT_[k]])

                  def stageB(i, hook=None):
                      hT_ = hTs[i % 2]
                      for r in range(22):
                          if hook is not None:
                              hook(r)
                          ys = []
                          for which in range(2):
                              ch = r + 22 * which
                              ps = kb.nxt("pup", ps_up)
                              for k in range(8):
                                  wg = wugrp[ch // 11]
                                  lc = (ch % 11) * 128
                                  kb.mm(ps[:, 0:FB + 2], wg[:, k, lc:lc + 128], hT_[k][:, 0:FB + 2], k == 0, k == 7,
                                        [wg, hT_[k]], [ps])
                              y = kb.nxt("yb", yb)
                              pc = 60 + ch * 4
                              kb.act(y[:, :], ps[:, 2:FB + 2], AF.Identity, [ps, par], [y], scale=par[:, pc + 2:pc + 3],
                                     bias=par[:, pc + 3:pc + 4])
                              kb.stt(y[:, :], ps[:, 1:FB + 1], par[:, pc + 1:pc + 2], y[:, :], ALU.mult, ALU.add, [ps, par, y], [y])
                              kb.stt(y[:, :], ps[:, 0:FB], par[:, pc:pc + 1], y[:, :], ALU.mult, ALU.add, [ps, par, y], [y])
                              ys.append(y)
                          g_ = kb.nxt("gl", gl)
                          kb.act(g_[:, :], ys[0][:, :], AF.Gelu_apprx_tanh, [ys[0]], [g_])
                          kb.tt("pool", aT[r][:, :], g_[:, :], ys[1][:, :], ALU.mult, [g_, ys[1]], [aT[r]])

                  def stageC(i):
                      c0 = i * FB
                      xb_ = xblks[i % 2]
                      for m in range(8):
                          ps = kb.nxt("pdn", ps_dn)
                          for r in range(22):
                              kb.mm(ps[:, 0:FB], wd[:, r, m * 128:(m + 1) * 128], aT[r][:, :], r == 0, r == 21, [wd, aT[r]], [ps])
                          kb.cp("act", yT[m][:, :], ps[:, 0:FB], [ps], [yT[m]])

                  def stageCt_sq(i):
                      kb.rms_sq([yT[m][:, :] for m in range(8)], yT, 128, FB, sqF)

                  def stageCt_fin(i):
                      c0 = i * FB
                      xb_ = xblks[i % 2]
                      rstd = kb.rms_fin(8, 128, D, FB, sqF)
                      for m in range(8):
                          kb.stt(yT[m][:, :], yT[m][:, :], par[:, 24 + m:25 + m], rstd[:, 0:FB], ALU.mult, ALU.mult,
                                 [yT[m], par, rstd], [yT[m]])
                          kb.tt("pool", yT[m][:, :], yT[m][:, :], xb_[:, m, :], ALU.add, [yT[m], xb_], [yT[m]])
                          kb.dma("sp", dst_ap[m * 128:(m + 1) * 128, c0:c0 + FB], yT[m][:, :], dst_b, [yT[m]])

                  stageA_ld(0)
                  stageA_sq(0)
                  stageA_fin(0)
                  for i in range(nfb):
                      def hook(r, i=i):
                          if r == 1 and i > 0:
                              stageCt_sq(i - 1)
                          if r == 5:
                              if i > 0:
                                  stageCt_fin(i - 1)
                              if i + 1 < nfb:
                                  stageA_ld(i + 1)
                          if r == 10 and i + 1 < nfb:
                              stageA_sq(i + 1)
                          if r == 15 and i + 1 < nfb:
                              stageA_fin(i + 1)
                      stageB(i, hook)
                      stageC(i)
                  stageCt_sq(nfb - 1)
                  stageCt_fin(nfb - 1)

        except _Stop:
            S.barrier()
            with CleanStack() as es:
                tcp = kb.sb(es, "tcp", [128, 8, 512], F32)
                kb.dma("sp", tcp[:, :, :], xT.rearrange("(k p) s -> p k s", p=128)[:, :, 0:512], tcp)
                kb.dma("sp", outT.rearrange("(k p) s -> p k s", p=128)[:, :, 0:512], tcp[:, :, :], B_out, [tcp])
        S.wait_all("sp", [B_out])
        S.barrier()
        keys = ["pe", "act", "dve", "pool"] + list(S.dma_bufs.keys())
        sems = {k: top.enter_context(nc.semaphore(f"s{j}")) for j, k in enumerate(keys)}
        kb.nsems = len(keys)
        with nc.Block() as block:
            @block.tensor
            def _(e):
                S.emit_one(e, "pe", sems)

            @block.scalar
            def _(e):
                S.emit_one(e, "act", sems)

            @block.vector
            def _(e):
                S.emit_one(e, "dve", sems)

            @block.gpsimd
            def _(e):
                S.emit_one(e, "pool", sems)

            @block.sync
            def _(e):
                S.emit_one(e, "sp", sems)
    return nc, kb


def _t5_bucket(dist):
    d = np.maximum(dist, 0)
    lr = np.log(np.maximum(d, 1).astype(np.float32) / np.float32(16)) / np.float32(math.log(128 / 16))
    large = 16 + (lr * 16).astype(np.int32)
    large = np.minimum(large, 31)
    return np.where(d < 16, d, large)


def host_prep(inp, S_, depth):
    f = np.float32
    L = depth
    w_in = np.asarray(inp["w_in"], f)[:L]
    A0, Fo, M0 = 0, 768, 1540
    cols1 = np.concatenate([
        np.arange(0, 512),
        np.arange(512, 576), np.arange(512, 576), np.arange(576, 640), np.arange(576, 640),
        np.arange(640, 768), Fo + np.arange(512, 768),
        Fo + np.arange(0, 256), Fo + np.arange(256, 512), Fo + np.arange(768, 772)])
    assert cols1.size == W1C
    wa1 = np.ascontiguousarray(w_in[:, :, cols1])
    kr = M0 + 384 + np.arange(32)
    krp = np.concatenate([kr[16:], kr[:16]])
    dmy = M0 + np.arange(64)
    cols2 = np.concatenate([M0 + np.arange(0, 384), dmy, kr, dmy, krp])
    assert cols2.size == W2C
    wa2 = np.ascontiguousarray(w_in[:, :, cols2])
    w_uq = np.asarray(inp["w_uq"], f)[:L]
    cq_ = []
    for h in range(4):
        cq_.append(np.arange(h * 96, (h + 1) * 96))
    for h in range(4):
        rp = h * 96 + 64 + np.arange(32)
        cq_.append(np.concatenate([np.arange(h * 96, h * 96 + 64), rp[16:], rp[:16]]))
    wuq = np.ascontiguousarray(w_uq[:, :, np.concatenate(cq_)])
    w_ukv = np.asarray(inp["w_ukv"], f)[:L]
    ck = np.concatenate([h * 128 + np.arange(64) for h in range(4)] + [h * 128 + 64 + np.arange(64) for h in range(4)])
    wukv = np.ascontiguousarray(w_ukv[:, :, ck])
    params = np.zeros((L, 128, NPAR), f)

    def pc(v):
        return v.reshape(L, -1, 128).transpose(0, 2, 1)

    params[:, :, 0:8] = pc(np.asarray(inp["attn_pre_norm"], f)[:L])
    params[:, :, 8:16] = pc(np.asarray(inp["attn_post_norm"], f)[:L])
    params[:, :, 16:24] = pc(np.asarray(inp["ffn_pre_norm"], f)[:L])
    params[:, :, 24:32] = pc(np.asarray(inp["ffn_post_norm"], f)[:L])
    gn = np.asarray(inp["group_norm"], f)[:L].reshape(L, 16, 64).transpose(0, 2, 1)
    params[:, 0:64, 32:48] = gn
    params[:, 64:128, 32:48] = gn
    params[:, :, 48:50] = pc(np.asarray(inp["q_latent_norm"], f)[:L])
    params[:, :, 50:51] = pc(np.asarray(inp["kv_latent_norm"], f)[:L])
    params[:, 0:4, 51] = np.asarray(inp["forget_bias"], f)[:L]
    params[:, :, 52:60] = np.asarray(inp["swa_sinks"], f)[:L][:, None, :]
    cw = np.asarray(inp["conv_w"], f)[:L]
    cb = np.asarray(inp["conv_b"], f)[:L]
    cv = np.concatenate([cw, cb[:, None, :]], axis=1)
    cv = cv.reshape(L, 4, 44, 128).transpose(0, 3, 2, 1)
    params[:, :, 60:] = cv.reshape(L, 128, 176)
    ident = np.eye(128, dtype=f)
    s_ = np.arange(128)[:, None]
    t_ = np.arange(128)[None, :]
    maskT = np.where(s_ <= t_, 0.0, -30000.0).astype(f)
    sel = np.zeros((4, 12, 128), f)
    for h in range(4):
        for p in range(3):
            sel[h, h * 3 + p, 64 + p] = 1.0
            sel[h, h * 3 + p, 96 + p] = 1.0
    pos = np.arange(S_, dtype=f)
    inv = (np.float32(10000.0) ** (-(np.arange(16, dtype=f) * np.float32(2.0) / np.float32(32)))).astype(f)
    ang = (pos[:, None] * inv[None, :]).astype(f)
    cs, sn = np.cos(ang).astype(f).T, np.sin(ang).astype(f).T
    rope = np.zeros((2, 128, S_), f)
    rope[0, 64:80] = cs
    rope[0, 80:96] = cs
    rope[1, 64:80] = -sn
    rope[1, 80:96] = sn
    rel = np.asarray(inp["rel_bias"], f)
    biasg = np.zeros((128, 2, 2, 4, 128), f)
    m01 = np.zeros((128, 2, 2, 4, 128), f)
    q_ = np.arange(128)[None, :]
    for pcx in range(2):
        dist = (q_ + 128 - s_) if pcx == 0 else (q_ - s_)
        ok = (dist >= 0) & (dist < 128)
        bk = _t5_bucket(np.clip(dist, 0, 127))
        for g in range(2):
            for hl in range(4):
                biasg[:, g, pcx, hl, :] = rel[bk, 4 * g + hl]
                m01[:, g, pcx, hl, :] = ok.astype(f)
    common = dict(wa1=wa1, wa2=wa2, wuq=wuq, wukv=wukv,
                  wout=np.ascontiguousarray(np.asarray(inp["w_out"], f)[:L]),
                  wup=np.ascontiguousarray(np.asarray(inp["w_up"], f)[:L]),
                  wdn=np.ascontiguousarray(np.asarray(inp["w_down"], f)[:L]),
                  params=params, c_ident=ident, c_mask=maskT, c_sel=sel.reshape(4, 12 * 128),
                  c_rope=rope, c_biasg=biasg.reshape(128, 2048), c_m01=m01.reshape(128, 2048))
    return common


_CACHE = {}


def run(inp, nblk=8, depth=DEPTH, ncores=8):
    S_ = nblk * 512
    key = (nblk, depth)
    if key not in _CACHE:
        _CACHE[key] = build_program(nblk, depth)[0]
    nc = _CACHE[key]
    common = host_prep(inp, S_, depth)
    x = np.asarray(inp["x"], np.float32)
    in_maps = []
    for b in range(ncores):
        m = dict(common)
        m["xT"] = np.ascontiguousarray(x[b, :S_, :].T)
        in_maps.append(m)
    res = run_bass_kernel_spmd(nc, in_maps, core_ids=list(range(ncores)))
    out = np.stack([np.ascontiguousarray(r["outT"].T) for r in res.results], axis=0)
    return out.astype(np.float32)


def kernel(**inputs):
    return run(inputs, nblk=SEQ // 512, depth=DEPTH, ncores=8)
```

```python
import math
import numpy as np
from contextlib import ExitStack
import concourse.bass as bass
import concourse.mybir as mybir
from concourse.bass_utils import run_bass_kernel_spmd

F32 = mybir.dt.float32
BF16 = mybir.dt.bfloat16
AF = mybir.ActivationFunctionType
ALU = mybir.AluOpType

D = 1024
DEPTH = 4
SEQ = 4096
NPAR = 60 + 176
EPS = 1e-6
W1C = 1668
QS, KD, TV, FQ, FK, FF = 0, 512, 768, 1152, 1408, 1664
W2C = 576
DFF = 2816
FB = 256

ENGS = ["pe", "act", "dve", "pool", "sp"]


class Buf:
    __slots__ = ("name", "w", "r", "dcnt", "key", "excl")

    def __init__(self, name):
        self.name = name
        self.excl = False
        self.w = None
        self.r = {}
        self.dcnt = 0
        self.key = ("dma", name)


class Tl:
    def __init__(self, t, name):
        self.t = t
        self.b = Buf(name)

    def __getitem__(self, idx):
        return self.t[idx]


def _b(x):
    return x.b if isinstance(x, Tl) else x


class Sched:
    def __init__(self):
        self.ops = {e: [] for e in ENGS}
        self.cnt = {e: 0 for e in ENGS}
        self.seen = {e: {} for e in ENGS}
        self.dma_bufs = {}

    def _deps(self, eng, reads, writes, own_key=None):
        need = {}
        for b in reads:
            if b.w is not None:
                k, v = b.w
                if need.get(k, 0) < v:
                    need[k] = v
            if b.excl:
                for k, v in b.r.items():
                    if k != eng and need.get(k, 0) < v:
                        need[k] = v
        for b in writes:
            if b.w is not None:
                k, v = b.w
                if need.get(k, 0) < v:
                    need[k] = v
            for k, v in b.r.items():
                if need.get(k, 0) < v:
                    need[k] = v
        waits = []
        seen = self.seen[eng]
        for k, v in need.items():
            if k == eng and eng == "pe":
                continue
            if own_key is not None and k == own_key:
                continue
            if k in self.dma_bufs:
                v = self.dma_bufs[k].dcnt
            if seen.get(k, 0) >= v:
                continue
            seen[k] = v
            waits.append((k, v))
        return waits

    def op(self, eng, fn, reads=(), writes=()):
        reads = [_b(x) for x in reads]
        writes = [_b(x) for x in writes]
        waits = self._deps(eng, reads, writes)
        self.cnt[eng] += 1
        v = self.cnt[eng]
        for b in reads:
            if b.r.get(eng, 0) < v:
                b.r[eng] = v
        for b in writes:
            b.w = (eng, v)
            b.r = {}
        self.ops[eng].append((waits, fn, eng, 1))

    def dma(self, q, fn, dst, reads=()):
        dst = _b(dst)
        reads = [_b(x) for x in reads]
        key = dst.key
        self.dma_bufs[key] = dst
        waits = self._deps(q, reads, [dst], own_key=key)
        dst.dcnt += 16
        for b in reads:
            if b.r.get(key, 0) < dst.dcnt:
                b.r[key] = dst.dcnt
        dst.w = (key, dst.dcnt)
        dst.r = {}
        self.ops[q].append((waits, fn, key, 16))

    def barrier(self):
        tot = {e: self.cnt[e] for e in ("pe", "act", "dve", "pool")}
        for k, b in self.dma_bufs.items():
            tot[k] = b.dcnt
        for e in ENGS:
            waits = []
            for k, v in tot.items():
                if k == e or v == 0:
                    continue
                if self.seen[e].get(k, 0) >= v:
                    continue
                self.seen[e][k] = v
                waits.append((k, v))
            if waits:
                self.ops[e].append((waits, None, None, 0))

    def wait_all(self, eng, bufs):
        waits = self._deps(eng, [_b(x) for x in bufs], [])
        self.ops[eng].append((waits, None, None, 0))

    def emit_one(self, eng, name, sems):
        for waits, fn, key, inc in self.ops[name]:
            for k, v in waits:
                eng.wait_ge(sems[k], v)
            if fn is not None:
                fn(eng).then_inc(sems[key], inc)


class KB:
    def __init__(self, nc, nblk, depth):
        self.nc = nc
        self.S = Sched()
        self.uid = 0
        self.nblk = nblk
        self.depth = depth
        self.rot = {}

    def sb(self, es, name, shape, dt):
        self.uid += 1
        nm = f"{name}_{self.uid}"
        return Tl(es.enter_context(self.nc.sbuf_tensor(nm, shape, dt)), nm)

    def nxt(self, key, lst):
        i = self.rot.get(key, 0)
        self.rot[key] = i + 1
        return lst[i % len(lst)]

    def mm(self, out, lhsT, rhs, start, stop, R, W):
        self.S.op("pe", lambda e: e.matmul(out, lhsT=lhsT, rhs=rhs, start=start, stop=stop), R, W)

    def act(self, out, in_, func, R, W, scale=None, bias=None):
        kw = {}
        if scale is not None:
            kw["scale"] = scale
        if bias is not None:
            kw["bias"] = bias
        self.S.op("act", lambda e: e.activation(out=out, in_=in_, func=func, **kw), R, W)

    def tt(self, eng, out, in0, in1, op, R, W):
        self.S.op(eng, lambda e: e.tensor_tensor(out=out, in0=in0, in1=in1, op=op), R, W)

    def ts(self, eng, out, in0, s1, s2, op0, op1, R, W):
        if op1 is None:
            self.S.op(eng, lambda e: e.tensor_scalar(out=out, in0=in0, scalar1=s1, scalar2=None, op0=op0), R, W)
        else:
            self.S.op(eng, lambda e: e.tensor_scalar(out=out, in0=in0, scalar1=s1, scalar2=s2, op0=op0, op1=op1), R, W)

    def stt(self, out, in0, scalar, in1, op0, op1, R, W):
        self.S.op("dve", lambda e: e.scalar_tensor_tensor(out=out, in0=in0, scalar=scalar, in1=in1, op0=op0, op1=op1), R, W)

    def cp(self, eng, out, in_, R, W):
        if eng == "act":
            self.S.op("act", lambda e: e.activation(out=out, in_=in_, func=AF.Copy), R, W)
        else:
            self.S.op(eng, lambda e: e.tensor_copy(out=out, in_=in_), R, W)

    def memset(self, eng, ap, val, W):
        self.S.op(eng, lambda e: e.memset(ap, val), (), W)

    def dma(self, q, out, in_, dst, R=()):
        self.S.dma(q, lambda e: e.dma_start(out=out, in_=in_), dst, R)

    def rms_sq(self, srcs, bufs, P, n, sqt, engs=("pool", "dve", "act", "pool", "dve", "pool", "dve", "act")):
        for j, (ap, b) in enumerate(zip(srcs, bufs)):
            sq = sqt[j]
            en = engs[j % len(engs)]
            if en == "act":
                self.act(sq[0:P, 0:n], ap, AF.Square, [b], [sq])
            else:
                self.tt(en, sq[0:P, 0:n], ap, ap, ALU.mult, [b], [sq])

    def rms_fin(self, nsrc, P, Dn, n, sqt):
        c = self.c
        ps = c["ps_stat"]
        for j in range(nsrc):
            sq = sqt[j]
            self.mm(ps[:, 0:n], c["ones_bf"][0:P, 0:128], sq[0:P, 0:n], j == 0, j == nsrc - 1,
                    [sq, c["ones_bf"]], [ps])
        lnv = c["lnv"]
        rstd = self.nxt("rstd", c["rstd"])
        self.act(lnv[:, 0:n], ps[:, 0:n], AF.Ln, [ps, c["epsb"]], [lnv], scale=1.0 / Dn, bias=c["epsb"][:, 0:1])
        self.act(rstd[:, 0:n], lnv[:, 0:n], AF.Exp, [lnv], [rstd], scale=-0.5)
        return rstd

    def rms(self, srcs, bufs, P, Dn, n, engs=("pool", "dve", "act", "pool", "dve", "pool", "dve", "act")):
        c = self.c
        ps = c["ps_stat"]
        for j, (ap, b) in enumerate(zip(srcs, bufs)):
            sq = self.nxt("sq", c["sq"])
            en = engs[j % len(engs)]
            if en == "act":
                self.act(sq[0:P, 0:n], ap, AF.Square, [b], [sq])
            else:
                self.tt(en, sq[0:P, 0:n], ap, ap, ALU.mult, [b], [sq])
            self.mm(ps[:, 0:n], c["ones_bf"][0:P, 0:128], sq[0:P, 0:n], j == 0, j == len(srcs) - 1,
                    [sq, c["ones_bf"]], [ps])
        lnv = c["lnv"]
        rstd = self.nxt("rstd", c["rstd"])
        self.act(lnv[:, 0:n], ps[:, 0:n], AF.Ln, [ps, c["epsb"]], [lnv], scale=1.0 / Dn, bias=c["epsb"][:, 0:1])
        self.act(rstd[:, 0:n], lnv[:, 0:n], AF.Exp, [lnv], [rstd], scale=-0.5)
        return rstd

    def proj_fm(self, w, col0, M, hT, n=512):
        ps = self.nxt("pp", self.c["ps_proj"])
        for k in range(8):
            self.mm(ps[0:M, 0:n], w[:, k, col0:col0 + M], hT[k][:, 0:n], k == 0, k == 7, [w, hT[k]], [ps])
        return ps

    def load_x(self, src, c0, n, xblk, srcB=()):
        self.dma("sp", xblk[:, :, 0:n], src.rearrange("(k p) s -> p k s", p=128)[:, :, c0:c0 + n], xblk, srcB)

    def prenorm(self, src, c0, n, xblk, hT, par, gcol, off=0, load=True):
        if load:
            self.load_x(src, c0, n, xblk)
        rstd = self.rms([xblk[:, k, 0:n] for k in range(8)], [xblk] * 8, 128, D, n)
        for k in range(8):
            self.stt(hT[k][:, off:off + n], xblk[:, k, 0:n], par[:, gcol + k:gcol + k + 1], rstd[:, 0:n],
                     ALU.mult, ALU.mult, [xblk, par, rstd], [hT[k]])

    def attn_family(self, i, kcs, qts, vc, KD_, scale, grp, hooks=None):
        c = self.c
        ntile = 4 * i + 4
        tasks = [(h, j) for h in range(4) for j in range(ntile)]
        T = len(tasks)
        L = 4
        st = {}
        pso = {}
        deferred = []
        for t in range(T + L + 5):
            if L <= t < T + L:
                h, j = tasks[t - L]
                ps_s, q0 = st.pop(t - L)
                if j == 0:
                    pso[h] = self.nxt("po", c["ps_o"])
                ps_o = pso[h]
                pt = self.nxt("pt", c["pt"])
                self.act(pt[:, q0:512], ps_s[:, q0:512], AF.Exp, [ps_s], [pt], scale=scale)
                self.mm(ps_o[0:65, q0:512], vc[:, j, h, 0:65], pt[:, q0:512], j == 0, j == ntile - 1, [vc, pt], [ps_o])
                if j == ntile - 1:
                    rd = self.finish_head(ps_o, None, 512)

                    def fin(ps_o=ps_o, rd=rd, h=h):
                        osb, ps_bc = self.finish_norm(ps_o, rd, 512)
                        self.tt("dve", grp[0:64, h, :], osb[0:64, :], ps_bc[0:64, :], ALU.mult, [osb, ps_bc], [grp])
                    deferred.append((t + 3, fin))
            if t < T:
                h, j = tasks[t]
                d = j - 4 * i
                q0 = 128 * d if d > 0 else 0
                ps_s = self.nxt("pss", c["ps_s"])
                self.mm(ps_s[:, q0:512], kcs[h][0:KD_, j * 128:(j + 1) * 128], qts[h][0:KD_, q0:512], True, d < 0,
                        [kcs[h], qts[h]], [ps_s])
                if d >= 0:
                    self.mm(ps_s[:, q0:q0 + 128], c["ident"][:, :], c["maskT"][:, :], False, True,
                            [c["ident"], c["maskT"]], [ps_s])
                st[t] = (ps_s, q0)
            while deferred and deferred[0][0] <= t:
                deferred.pop(0)[1]()
            if hooks and t in hooks:
                hooks[t]()
        assert not deferred and not st

    def finish_head(self, ps_o, sink_ap, n, col0=0, rd=None):
        c = self.c
        lnd = c["lnd"]
        if rd is None:
            rd = self.nxt("rd", c["rd"])
        sl = slice(col0, col0 + n)
        if sink_ap is None:
            self.act(lnd[64:65, sl], ps_o[64:65, sl], AF.Ln, [ps_o], [lnd])
        else:
            self.act(lnd[64:65, sl], ps_o[64:65, sl], AF.Ln, [ps_o, c["esink"]], [lnd], bias=sink_ap)
        self.act(rd[64:65, sl], lnd[64:65, sl], AF.Exp, [lnd], [rd], scale=-1.0)
        return rd

    def finish_norm(self, ps_o, rd, n, cpeng="dve"):
        c = self.c
        ps_bc = c["ps_bc"]
        rb = self.nxt("rdb", c["rdb"])
        self.cp("dve", rb[64:65, 0, 0:n], rd[64:65, 0:n], [rd], [rb])
        self.stt(rb[64:65, 1, 0:n], rb[64:65, 0, 0:n], -1.0, rd[64:65, 0:n], ALU.mult, ALU.add, [rb, rd], [rb])
        self.mm(ps_bc[0:64, 0:n], c["sel64"][:, 0:64], rb[:, 0, 0:n], True, False, [c["sel64"], rb], [ps_bc])
        self.mm(ps_bc[0:64, 0:n], c["sel64"][:, 0:64], rb[:, 1, 0:n], False, True, [c["sel64"], rb], [ps_bc])
        osb = self.nxt("osb", c["osb"])
        self.cp(cpeng, osb[0:64, 0:n], ps_o[0:64, 0:n], [ps_o], [osb])
        return osb, ps_bc

    def evac(self, out, in_, R, W):
        k = self.rot.get("evac", 0)
        self.rot["evac"] = k + 1
        self.cp("act" if k % 4 == 3 else "dve", out, in_, R, W)


class _Stop(Exception):
    pass


class CleanStack(ExitStack):
    def __exit__(self, *exc):
        super().__exit__(None, None, None)
        return False


def build_program(nblk=8, depth=DEPTH, upto=None):
    S_ = nblk * 512
    NT = S_ // 128
    nc = bass.Bass("TRN2", target_bir_lowering=False)

    def din(name, shape, dt=F32):
        return nc.dram_tensor(name, shape, dt, kind="ExternalInput").ap()

    xT = din("xT", [D, S_])
    wa1 = din("wa1", [depth, D, W1C])
    wa2 = din("wa2", [depth, D, W2C])
    wuq = din("wuq", [depth, 256, 768])
    wukv = din("wukv", [depth, 128, 512])
    wout = din("wout", [depth, D, D])
    wup = din("wup", [depth, D, 2 * DFF])
    wdn = din("wdn", [depth, DFF, D])
    params = din("params", [depth, 128, NPAR])
    c_ident = din("c_ident", [128, 128])
    c_mask = din("c_mask", [128, 128])
    c_sel = din("c_sel", [4, 12 * 128])
    c_rope = din("c_rope", [2, 128, S_])
    c_biasg = din("c_biasg", [128, 2048])
    c_m01 = din("c_m01", [128, 2048])
    outT = nc.dram_tensor("outT", [D, S_], F32, kind="ExternalOutput").ap()
    xa = nc.dram_tensor("xa", [D, S_], F32, kind="Internal").ap()
    xb = nc.dram_tensor("xb", [D, S_], F32, kind="Internal").ap()
    mixab = nc.dram_tensor("mixab", [768, S_], BF16, kind="Internal").ap()
    B_xa, B_xb, B_mix, B_out = Buf("xa"), Buf("xb"), Buf("mixab"), Buf("outT")

    kb = KB(nc, nblk, depth)
    S = kb.S
    with CleanStack() as top:
        c = kb.c = {}
        c["ones_bf"] = kb.sb(top, "ones_bf", [128, 128], BF16)
        c["ones_f"] = kb.sb(top, "ones_f", [128, 128], F32)
        c["ident"] = kb.sb(top, "ident", [128, 128], BF16)
        c["maskT"] = kb.sb(top, "maskT", [128, 128], BF16)
        c["epsb"] = kb.sb(top, "epsb", [128, 2], F32)
        c["sq"] = [kb.sb(top, "sq", [128, 512], BF16) for _ in range(3)]
        c["lnv"] = kb.sb(top, "lnv", [128, 512], F32)
        c["rstd"] = [kb.sb(top, "rstd", [128, 512], F32) for _ in range(2)]
        c["esink"] = kb.sb(top, "esink", [128, 8], F32)
        pst = [Tl(top.enter_context(nc.psum_tensor(f"psum{j}", [128, 512], F32)), f"psum{j}") for j in range(8)]
        for p_ in pst:
            p_.b.excl = True
        c["ps_stat"] = pst[0]
        c["ps_proj"] = [pst[1], pst[2], pst[3], pst[4], pst[5], pst[6]]
        c["ps_s"] = [pst[3], pst[4], pst[1], pst[2]]
        c["ps_o"] = [pst[5], pst[6]]
        c["ps_bc"] = pst[7]

        kb.memset("dve", c["ones_bf"][:, :], 1.0, [c["ones_bf"]])
        c["sel64"] = kb.sb(top, "sel64", [128, 64], BF16)
        kb.memset("dve", c["sel64"][:, :], 0.0, [c["sel64"]])
        kb.memset("dve", c["sel64"][64:65, :], 1.0, [c["sel64"]])
        kb.memset("dve", c["ones_f"][:, :], 1.0, [c["ones_f"]])
        kb.memset("dve", c["epsb"][:, 0:1], EPS, [c["epsb"]])
        kb.dma("pool", c["ident"][:, :], c_ident, c["ident"])
        kb.dma("pool", c["maskT"][:, :], c_mask, c["maskT"])

        def stop(tag):
            if upto == tag:
                raise _Stop()

        try:
          stop("const")
          for l in range(depth):
              src = xT if l == 0 else xb
              srcB = [] if l == 0 else [B_xb]
              last = (l == depth - 1)
              S.barrier()
              with CleanStack() as es:
                  c["lnd"] = kb.sb(es, "lnd", [65, 512], F32)
                  c["rd"] = [kb.sb(es, "rd", [65, 512], F32) for _ in range(2)]
                  c["rdb"] = [kb.sb(es, "rdb", [128, 2, 512], BF16) for _ in range(1)]
                  kb.memset("pool", c["rdb"][0][:, :, :], 0.0, [c["rdb"][0]])
                  c["osb"] = [kb.sb(es, "osb", [64, 512], F32) for _ in range(2)]
                  c["pt"] = [kb.sb(es, "pt", [128, 512], BF16) for _ in range(5)]
                  c["sel"] = kb.sb(es, "sel", [4, 12 * 128], BF16)
                  c["zeros4"] = kb.sb(es, "zeros4", [4, 512], F32)
                  kb.memset("dve", c["zeros4"][:, :], 0.0, [c["zeros4"]])
                  kb.dma("pool", c["sel"][:, :], c_sel, c["sel"])
                  w1 = kb.sb(es, "w1", [128, 8, W1C], BF16)
                  par = kb.sb(es, "par", [128, NPAR], F32)
                  for k in range(8):
                      kb.dma("pool", w1[:, k, :], wa1[l, k * 128:(k + 1) * 128, :], w1)
                  kb.dma("sp", par[:, :], params[l], par)
                  kaug = [kb.sb(es, f"kaug{h}", [99, S_], BF16) for h in range(4)]
                  qaug = [kb.sb(es, f"qaug{h}", [99, 512], BF16) for h in range(4)]
                  vfox = kb.sb(es, "vfox", [128, NT, 4, 72], BF16)
                  qsw = [kb.sb(es, f"qsw{p}", [128, 512], BF16) for p in range(4)]
                  kdup = [[kb.sb(es, f"kdup{g}{ab}", [128, 1024], BF16) for ab in range(2)] for g in range(2)]
                  for g in range(2):
                      kb.memset("pool", kdup[g][0][64:128, :], 0.0, [kdup[g][0]])
                      kb.memset("pool", kdup[g][1][0:64, :], 0.0, [kdup[g][1]])
                  vsw = kb.sb(es, "vsw", [128, 8, 2, 72], BF16)
                  xblk = kb.sb(es, "xblk", [128, 8, 512], F32)
                  hT = [kb.sb(es, f"hT{k}", [128, 512], BF16) for k in range(8)]
                  grpA = kb.sb(es, "grpA", [64, 8, 512], F32)
                  grpB = grpA
                  mixt = [kb.sb(es, "mixt", [64, 12, 512], BF16) for _ in range(1)]
                  btab = kb.sb(es, "btab", [128, 2048], BF16)
                  kb.dma("sp", xblk[:, 0:4, :], c_biasg.rearrange("p (a b) -> p a b", a=4), xblk)
                  kb.dma("sp", xblk[:, 4:8, :], c_m01.rearrange("p (a b) -> p a b", a=4), xblk)
                  for q_ in range(4):
                      kb.ts("dve", xblk[:, q_, :], xblk[:, q_, :], 8.0, 30000.0, ALU.mult, ALU.add, [xblk], [xblk])
                      kb.tt("dve", xblk[:, q_, :], xblk[:, q_, :], xblk[:, 4 + q_, :], ALU.mult, [xblk], [xblk])
                      kb.ts("dve", btab[:, q_ * 512:(q_ + 1) * 512], xblk[:, q_, :], -30000.0, None, ALU.add, None, [xblk], [btab])
                  pmt = [kb.sb(es, "pmt", [128, 512], BF16) for _ in range(4)]
                  negfb = kb.sb(es, "negfb", [4, 1], F32)
                  e1 = kb.sb(es, "e1", [4, 512], F32)
                  nlf = e1
                  Gt = [kb.sb(es, "G", [4, 512], F32) for _ in range(2)]
                  r1 = kb.sb(es, "r1", [4, 512], F32)
                  r2 = r1
                  g3 = kb.sb(es, "g3", [4, 3, 512], BF16)

                  for h in range(4):
                      kb.memset("pool", kaug[h][64:99, :], 0.0, [kaug[h]])
                      kb.memset("pool", kaug[h][96:99, :], -8.0, [kaug[h]])
                      kb.memset("pool", qaug[h][64:99, :], 0.0, [qaug[h]])
                      kb.memset("pool", qaug[h][64:67, :], 8.0, [qaug[h]])
                  kb.memset("pool", vfox[:, :, :, :], 1.0, [vfox])
                  kb.memset("pool", vsw[:, :, :, :], 1.0, [vsw])
                  kb.ts("dve", negfb[0:4, 0:1], par[0:4, 51:52], -1.0, None, ALU.mult, None, [par], [negfb])
                  kb.act(c["esink"][:, :], par[:, 52:60], AF.Exp, [par], [c["esink"]])
                  skb = kb.sb(es, "skb", [128, 1024], BF16)
                  e64 = kb.sb(es, "e64", [128, 72], BF16)
                  kb.memset("pool", skb[:, :], 0.0, [skb])
                  kb.memset("pool", e64[:, :], 0.0, [e64])
                  kb.memset("pool", e64[0:1, 64:65], 1.0, [e64])
                  kb.memset("pool", e64[32:33, 64:65], 1.0, [e64])
                  for half, skf in enumerate((c["lnv"], c["rstd"][0])):
                      for hh in range(4):
                          h = half * 4 + hh
                          for p0 in (0, 32):
                              kb.ts("dve", skf[p0:p0 + 1, hh * 128:(hh + 1) * 128], c["ones_f"][p0:p0 + 1, 0:128],
                                    c["esink"][p0:p0 + 1, h:h + 1], None, ALU.mult, None, [c["ones_f"], c["esink"]], [skf])
                      hs = slice(half * 512, half * 512 + 512)
                      kb.cp("dve", skb[0:1, hs], skf[0:1, :], [skf], [skb])
                      kb.cp("dve", skb[32:33, hs], skf[32:33, :], [skf], [skb])
                      kb.tt("dve", skf[32:33, :], skf[32:33, :], skb[32:33, hs], ALU.subtract, [skf, skb], [skf])
                      kb.cp("dve", skb[32:33, hs], skf[32:33, :], [skf], [skb])
                  stop("a1load")

                  acc = kb.sb(es, "acc", [128, 512], F32)
                  tmpq = kb.sb(es, "tmpq", [128, 512], BF16)

                  def pn_acc():
                      for k in range(8):
                          kb.tt("pool", tmpq[:, :], xblk[:, k, :], xblk[:, k, :], ALU.mult, [xblk], [tmpq])
                          if k == 0:
                              kb.cp("pool", acc[:, :], tmpq[:, :], [tmpq], [acc])
                          else:
                              kb.tt("pool", acc[:, :], acc[:, :], tmpq[:, :], ALU.add, [acc, tmpq], [acc])

                  def pn_fin():
                      ps = c["ps_stat"]
                      kb.mm(ps[:, :], c["ones_f"][:, 0:128], acc[:, :], True, True, [c["ones_f"], acc], [ps])
                      lnv = c["lnv"]
                      rstd = kb.nxt("rstd", c["rstd"])
                      kb.act(lnv[:, :], ps[:, :], AF.Ln, [ps, c["epsb"]], [lnv], scale=1.0 / D, bias=c["epsb"][:, 0:1])
                      kb.act(rstd[:, :], lnv[:, :], AF.Exp, [lnv], [rstd], scale=-0.5)
                      for k in range(8):
                          kb.stt(hT[k][:, :], xblk[:, k, :], par[:, k:k + 1], rstd[:, :], ALU.mult, ALU.mult,
                                 [xblk, par, rstd], [hT[k]])

                  def pn_next(nxt_blk):
                      pn_fin()
                      if nxt_blk + 1 < nblk:
                          kb.load_x(src, (nxt_blk + 1) * 512, 512, xblk)
                          pn_acc()

                  kb.load_x(src, 0, 512, xblk)
                  pn_acc()
                  pn_next(0)
                  for i in range(nblk):
                      c0 = i * 512
                      ps = kb.proj_fm(w1, FF, 4, hT)
                      kb.act(e1[0:4, :], ps[0:4, :], AF.Exp, [ps, negfb], [e1], scale=-1.0, bias=negfb[0:4, 0:1])
                      kb.act(nlf[0:4, :], e1[0:4, :], AF.Ln, [e1, c["ones_f"]], [nlf], bias=c["ones_f"][0:4, 0:1])
                      G = Gt[i % 2]
                      Gp = Gt[(i + 1) % 2]
                      init = 0.0 if i == 0 else Gp[0:4, 511:512]
                      S.op("dve", (lambda G=G, init=init, nlf=nlf, z4=c["zeros4"]: lambda e: e.tensor_tensor_scan(
                          out=G[0:4, :], data0=z4[0:4, :], data1=nlf[0:4, :], initial=init,
                          op0=ALU.add, op1=ALU.add))(), [c["zeros4"], nlf, Gp], [G])
                      kb.cp("dve", g3[0:4, 0, :], G[0:4, :], [G], [g3])
                      kb.tt("dve", r1[0:4, :], G[0:4, :], g3[0:4, 0, :], ALU.subtract, [G, g3], [r1])
                      kb.cp("dve", g3[0:4, 1, :], r1[0:4, :], [r1], [g3])
                      kb.tt("dve", r2[0:4, :], r1[0:4, :], g3[0:4, 1, :], ALU.subtract, [r1, g3], [r2])
                      kb.cp("dve", g3[0:4, 2, :], r2[0:4, :], [r2], [g3])
                      stop("a1gate")
                      for p in range(4):
                          ps = kb.proj_fm(w1, QS + p * 128, 128, hT)
                          kb.evac(qsw[p][:, :], ps[:, :], [ps], [qsw[p]])
                      stop("p1")
                      rb = (i % 2) * 512
                      for g in range(2):
                          ps = kb.proj_fm(w1, KD + g * 128, 128, hT)
                          kb.cp("dve", kdup[g][0][0:64, rb:rb + 512], ps[0:64, :], [ps], [kdup[g][0]])
                          kb.cp("dve", kdup[g][1][64:128, rb:rb + 512], ps[64:128, :], [ps], [kdup[g][1]])
                      stop("p2")
                      for t4 in range(4):
                          ps = kb.nxt("pp", c["ps_proj"])
                          for k in range(8):
                              kb.mm(ps[:, 0:384], hT[k][:, t4 * 128:(t4 + 1) * 128], w1[:, k, TV:TV + 384], k == 0, k == 7,
                                    [hT[k], w1], [ps])
                          stop("p2a")
                          n = 4 * i + t4
                          for g in range(2):
                              kb.cp("act", vsw[:, n % 8, g, 0:64], ps[:, g * 64:(g + 1) * 64], [ps], [vsw])
                          stop("p2b")
                          for h in range(4):
                              kb.cp("dve", vfox[:, n, h, 0:64], ps[:, 128 + h * 64:128 + (h + 1) * 64], [ps], [vfox])
                          stop("p2c")
                          if t4 == 1:
                              stop("p2d")
                      stop("p3")
                      for h in range(4):
                          ps = kb.proj_fm(w1, FQ + h * 64, 64, hT)
                          kb.evac(qaug[h][0:64, :], ps[0:64, :], [ps], [qaug[h]])
                          ps = kb.proj_fm(w1, FK + h * 64, 64, hT)
                          kb.evac(kaug[h][0:64, c0:c0 + 512], ps[0:64, :], [ps], [kaug[h]])
                          stop("p4")
                          ps = kb.nxt("pp", c["ps_proj"])
                          for part in range(3):
                              o = (h * 3 + part) * 128
                              kb.mm(ps[0:99, :], c["sel"][0:4, o:o + 99], g3[0:4, part, :], part == 0, part == 2,
                                    [c["sel"], g3], [ps])
                          stop("p5")
                          kb.cp("dve", kaug[h][64:67, c0:c0 + 512], ps[64:67, :], [ps], [kaug[h]])
                          kb.cp("dve", qaug[h][96:99, :], ps[96:99, :], [ps], [qaug[h]])
                      stop("a1proj")
                      its = [(g, qt) for g in range(2) for qt in range(4)]
                      pend = {}

                      def s1(k):
                          g, qt = its[k]
                          n = 4 * i + qt
                          cur = rb + qt * 128
                          prv = (cur - 128) % 1024
                          pcs = ([0] if n > 0 else []) + [1]
                          pms = []
                          for pc in pcs:
                              kc = prv if pc == 0 else cur
                              ps_s = kb.nxt("pss", c["ps_s"])
                              eo = (g * 2 + pc) * 512
                              kb.mm(ps_s[:, :], c["ident"][:, :], btab[:, eo:eo + 512], True, False, [c["ident"], btab], [ps_s])
                              for hl in range(4):
                                  h = 4 * g + hl
                                  kd = kdup[g][h % 2]
                                  kb.mm(ps_s[:, hl * 128:(hl + 1) * 128], kd[:, kc:kc + 128],
                                        qsw[h // 2][:, qt * 128:(qt + 1) * 128], False, hl == 3,
                                        [kd, qsw[h // 2]], [ps_s])
                              pm = kb.nxt("pmt", pmt)
                              kb.act(pm[:, :], ps_s[:, :], AF.Exp, [ps_s], [pm], scale=0.125)
                              pms.append((pm, (n - 1 + pc) % 8))
                          pend[k] = pms

                      def s2a(k):
                          g, qt = its[k]
                          pms = pend.pop(k)
                          ps_o = kb.nxt("po", c["ps_o"])
                          for hl in range(4):
                              for idx, (pm, slot) in enumerate(pms):
                                  kb.mm(ps_o[0:65, hl * 128:(hl + 1) * 128], vsw[:, slot, g, 0:65],
                                        pm[:, hl * 128:(hl + 1) * 128], idx == 0, False,
                                        [vsw, pm], [ps_o])
                              so = g * 512 + hl * 128
                              kb.mm(ps_o[0:65, hl * 128:(hl + 1) * 128], e64[:, 0:65], skb[:, so:so + 128], False, True,
                                    [e64, skb], [ps_o])
                          rd = kb.finish_head(ps_o, None, 512)
                          pend[("o", k)] = (ps_o, rd)

                      def s2b(k):
                          g, qt = its[k]
                          ps_o, rd = pend.pop(("o", k))
                          osb, ps_bc = kb.finish_norm(ps_o, rd, 512, cpeng="act")
                          for hl in range(4):
                              h = 4 * g + hl
                              kb.tt("dve", grpA[0:64, h, qt * 128:(qt + 1) * 128], osb[0:64, hl * 128:(hl + 1) * 128],
                                    ps_bc[0:64, hl * 128:(hl + 1) * 128], ALU.mult, [osb, ps_bc], [grpA])

                      for k in range(len(its) + 3):
                          if k < len(its):
                              s1(k)
                          if 3 <= k:
                              s2b(k - 3)
                          if 1 <= k <= len(its):
                              s2a(k - 1)
                      mx = mixt[0]
                      rstd = kb.rms([grpA[0:64, h, :] for h in range(8)], [grpA] * 8, 64, 512, 512)
                      for h in range(8):
                          kb.stt(mx[0:64, h, :], grpA[0:64, h, :], par[0:64, 32 + h:33 + h], rstd[0:64, :],
                                 ALU.mult, ALU.mult, [grpA, par, rstd], [mx])
                      stop("a1swa")
                      hk = None
                      if i + 1 < nblk:
                          hk = {4 + 2 * (4 * i + 4): (lambda i=i: pn_next(i + 1))}
                      kb.attn_family(i, kaug, qaug, vfox, 99, 0.125, grpB, hooks=hk)
                      rstd = kb.rms([grpB[0:64, h, :] for h in range(4)], [grpB] * 4, 64, 256, 512)
                      for h in range(4):
                          kb.stt(mx[0:64, 8 + h, :], grpB[0:64, h, :], par[0:64, 40 + h:41 + h], rstd[0:64, :],
                                 ALU.mult, ALU.mult, [grpB, par, rstd], [mx])
                      kb.dma("sp", mixab.rearrange("(t p) s -> p t s", p=64)[:, :, c0:c0 + 512], mx[0:64, :, :], B_mix, [mx])

              stop("a1")
              S.barrier()
              with CleanStack() as es:
                  c["lnd"] = kb.sb(es, "lnd", [65, 512], F32)
                  c["rd"] = [kb.sb(es, "rd", [65, 512], F32) for _ in range(1)]
                  c["rdb"] = [kb.sb(es, "rdb", [128, 2, 512], BF16) for _ in range(1)]
                  kb.memset("pool", c["rdb"][0][:, :, :], 0.0, [c["rdb"][0]])
                  c["osb"] = [kb.sb(es, "osb", [64, 512], F32) for _ in range(2)]
                  c["pt"] = [kb.sb(es, "pt", [128, 512], BF16) for _ in range(5)]
                  w2 = kb.sb(es, "w2", [128, 8, W2C], BF16)
                  wq = kb.sb(es, "wq", [128, 2, 768], BF16)
                  wkv = kb.sb(es, "wkv", [128, 512], BF16)
                  wo = kb.sb(es, "wo", [128, 6, D], BF16)
                  woc = kb.sb(es, "woc", [64, 4, D], BF16)
                  par = kb.sb(es, "par", [128, NPAR], F32)
                  for k in range(8):
                      kb.dma("pool", w2[:, k, :], wa2[l, k * 128:(k + 1) * 128, :], w2)
                  for k in range(2):
                      kb.dma("pool", wq[:, k, :], wuq[l, k * 128:(k + 1) * 128, :], wq)
                  kb.dma("pool", wkv[:, :], wukv[l], wkv)
                  for r in range(6):
                      kb.dma("pool", wo[:, r, :], wout[l, r * 128:(r + 1) * 128, :], wo)
                  for r in range(4):
                      kb.dma("pool", woc[0:64, r, :], wout[l, 768 + r * 64:768 + (r + 1) * 64, :], woc)
                  kb.dma("sp", par[:, :], params[l], par)
                  kmla = [kb.sb(es, f"kmla{h}", [96, S_], BF16) for h in range(4)]
                  qmla = [kb.sb(es, f"qmla{h}", [96, 512], BF16) for h in range(4)]
                  vmla = kb.sb(es, "vmla", [128, NT, 4, 72], BF16)
                  xblk = kb.sb(es, "xblk", [128, 8, 512], F32)
                  hT = [kb.sb(es, f"hT{k}", [128, 512], BF16) for k in range(8)]
                  cq = [kb.sb(es, f"cq{k}", [128, 512], F32) for k in range(2)]
                  cqn = [kb.sb(es, f"cqn{k}", [128, 512], BF16) for k in range(2)]
                  ckv = kb.sb(es, "ckv", [128, 512], F32)
                  ckvn = kb.sb(es, "ckvn", [128, 512], BF16)
                  rope = kb.sb(es, "rope", [128, 2, 512], F32)
                  tm1 = [kb.sb(es, "tm1", [96, 512], F32) for _ in range(1)]
                  tm2 = [kb.sb(es, "tm2", [96, 512], F32) for _ in range(1)]
                  grpC = kb.sb(es, "grpC", [64, 4, 512], F32)
                  mixc = [kb.sb(es, f"mixc{h}", [64, 512], BF16) for h in range(4)]
                  mab = kb.sb(es, "mab", [128, 6, 512], BF16)
                  yT = [kb.sb(es, f"yT{m}", [128, 512], F32) for m in range(8)]
                  kb.memset("pool", vmla[:, :, :, :], 1.0, [vmla])
                  sc_mla = 96.0 ** -0.5

                  xblks2 = [xblk, kb.sb(es, "xblk2", [128, 8, 512], F32)]
                  def a2_tail(i):
                      c0 = i * 512
                      xb_ = xblks2[i % 2]
                      rstd = kb.rms([yT[m][:, :] for m in range(8)], yT, 128, D, 512)
                      for m in range(8):
                          kb.stt(yT[m][:, :], yT[m][:, :], par[:, 8 + m:9 + m], rstd[:, :], ALU.mult, ALU.mult,
                                 [yT[m], par, rstd], [yT[m]])
                          kb.tt("pool", yT[m][:, :], yT[m][:, :], xb_[:, m, :], ALU.add, [yT[m], xb_], [yT[m]])
                          kb.dma("sp", xa[m * 128:(m + 1) * 128, c0:c0 + 512], yT[m][:, :], B_xa, [yT[m]])

                  kb.load_x(src, 0, 512, xblks2[0])
                  kb.prenorm(src, 0, 512, xblks2[0], hT, par, 0, load=False)
                  for i in range(nblk):
                      c0 = i * 512
                      xblk = xblks2[i % 2]
                      kb.dma("sp", rope[64:96, 0, :], c_rope[0, 64:96, c0:c0 + 512], rope)
                      kb.dma("sp", rope[64:96, 1, :], c_rope[1, 64:96, c0:c0 + 512], rope)
                      kb.dma("sp", mab[:, :, :], mixab.rearrange("(t p) s -> p t s", p=128)[:, :, c0:c0 + 512], mab, [B_mix])
                      for k in range(2):
                          ps = kb.proj_fm(w2, k * 128, 128, hT)
                          kb.evac(cq[k][:, :], ps[:, :], [ps], [cq[k]])
                      ps = kb.proj_fm(w2, 256, 128, hT)
                      kb.evac(ckv[:, :], ps[:, :], [ps], [ckv])
                      rstd = kb.rms([cq[0][:, :], cq[1][:, :]], cq, 128, 256, 512)
                      for k in range(2):
                          kb.stt(cqn[k][:, :], cq[k][:, :], par[:, 48 + k:49 + k], rstd[:, :], ALU.mult, ALU.mult,
                                 [cq[k], par, rstd], [cqn[k]])
                      rstd = kb.rms([ckv[:, :]], [ckv], 128, 128, 512)
                      kb.stt(ckvn[:, :], ckv[:, :], par[:, 50:51], rstd[:, :], ALU.mult, ALU.mult, [ckv, par, rstd], [ckvn])
                      psa = kb.proj_fm(w2, 384, 96, hT)
                      psb = kb.proj_fm(w2, 480, 96, hT)
                      t1 = kb.nxt("tm1", tm1)
                      t2 = kb.nxt("tm2", tm2)
                      kb.tt("dve", t1[64:96, :], psa[64:96, :], rope[64:96, 0, :], ALU.mult, [psa, rope], [t1])
                      kb.tt("dve", t2[64:96, :], psb[64:96, :], rope[64:96, 1, :], ALU.mult, [psb, rope], [t2])
                      for h in range(4):
                          kb.tt("pool" if h % 2 else "dve", kmla[h][64:96, c0:c0 + 512], t1[64:96, :], t2[64:96, :], ALU.add,
                                [t1, t2], [kmla[h]])
                      for h in range(4):
                          ps = kb.nxt("pp", c["ps_proj"])
                          kb.mm(ps[0:64, :], wkv[:, h * 64:(h + 1) * 64], ckvn[:, :], True, True, [wkv, ckvn], [ps])
                          kb.evac(kmla[h][0:64, c0:c0 + 512], ps[0:64, :], [ps], [kmla[h]])
                      for t4 in range(4):
                          ps = kb.nxt("pp", c["ps_proj"])
                          kb.mm(ps[:, 0:256], ckvn[:, t4 * 128:(t4 + 1) * 128], wkv[:, 256:512], True, True, [ckvn, wkv], [ps])
                          for h in range(4):
                              kb.cp("dve" if h % 2 else "act", vmla[:, 4 * i + t4, h, 0:64], ps[:, h * 64:(h + 1) * 64], [ps], [vmla])
                      if i > 0:
                          a2_tail(i - 1)
                      if i + 1 < nblk:
                          kb.load_x(src, c0 + 512, 512, xblks2[(i + 1) % 2])
                      for h in range(4):
                          psa = kb.nxt("pp", c["ps_proj"])
                          for k in range(2):
                              kb.mm(psa[0:96, :], wq[:, k, h * 96:(h + 1) * 96], cqn[k][:, :], k == 0, k == 1, [wq, cqn[k]], [psa])
                          psb = kb.nxt("pp", c["ps_proj"])
                          for k in range(2):
                              kb.mm(psb[0:96, :], wq[:, k, 384 + h * 96:384 + (h + 1) * 96], cqn[k][:, :], k == 0, k == 1,
                                    [wq, cqn[k]], [psb])
                          kb.cp("dve", qmla[h][0:64, :], psa[0:64, :], [psa], [qmla[h]])
                          t1 = kb.nxt("tm1", tm1)
                          t2 = kb.nxt("tm2", tm2)
                          kb.tt("dve", t1[64:96, :], psa[64:96, :], rope[64:96, 0, :], ALU.mult, [psa, rope], [t1])
                          kb.tt("dve", t2[64:96, :], psb[64:96, :], rope[64:96, 1, :], ALU.mult, [psb, rope], [t2])
                          kb.tt("pool", qmla[h][64:96, :], t1[64:96, :], t2[64:96, :], ALU.add, [t1, t2], [qmla[h]])
                      kb.attn_family(i, kmla, qmla, vmla, 96, sc_mla, grpC)
                      rstd = kb.rms([grpC[0:64, h, :] for h in range(4)], [grpC] * 4, 64, 256, 512)
                      for h in range(4):
                          kb.stt(mixc[h][0:64, :], grpC[0:64, h, :], par[0:64, 44 + h:45 + h], rstd[0:64, :],
                                 ALU.mult, ALU.mult, [grpC, par, rstd], [mixc[h]])
                      for m in range(8):
                          ps = kb.nxt("pp", c["ps_proj"])
                          for r in range(6):
                              kb.mm(ps[:, :], wo[:, r, m * 128:(m + 1) * 128], mab[:, r, :], r == 0, False, [wo, mab], [ps])
                          for r in range(4):
                              kb.mm(ps[:, :], woc[0:64, r, m * 128:(m + 1) * 128], mixc[r][0:64, :], False, r == 3,
                                    [woc, mixc[r]], [ps])
                          kb.cp("act", yT[m][:, :], ps[:, :], [ps], [yT[m]])
                      if i + 1 < nblk:
                          kb.prenorm(src, c0 + 512, 512, xblks2[(i + 1) % 2], hT, par, 0, load=False)
                  a2_tail(nblk - 1)

              stop("a2")
              S.barrier()
              with CleanStack() as es:
                  wugrp = [kb.sb(es, f"wu{q4}", [128, 8, 1408], BF16) for q4 in range(4)]
                  wd = kb.sb(es, "wd", [128, 22, D], BF16)
                  par = kb.sb(es, "par", [128, NPAR], F32)
                  kb.dma("sp", par[:, :], params[l], par)
                  for q4 in (0, 2, 1, 3):
                      for k in range(8):
                          kb.dma("pool", wugrp[q4][:, k, :],
                                 wup[l, k * 128:(k + 1) * 128, q4 * 1408:(q4 + 1) * 1408], wugrp[q4])
                  for r in range(22):
                      kb.dma("pool", wd[:, r, :], wdn[l, r * 128:(r + 1) * 128, :], wd)
                  xblks = [kb.sb(es, "xblk", [128, 8, FB], F32) for _ in range(2)]
                  hTs = [[kb.sb(es, f"hT{k}", [128, FB + 2], BF16) for k in range(8)] for _ in range(2)]
                  aT = [kb.sb(es, f"aT{r}", [128, FB], BF16) for r in range(22)]
                  yT = [kb.sb(es, f"yT{m}", [128, FB], F32) for m in range(8)]
                  yb = [kb.sb(es, "yb", [128, FB], F32) for _ in range(6)]
                  gl = [kb.sb(es, "gl", [128, FB], F32) for _ in range(2)]
                  ps_up = [pst[j] for j in (1, 2, 3, 4, 5)]
                  ps_dn = [pst[6], pst[7]]
                  for k in range(8):
                      kb.memset("pool", hTs[0][k][:, 0:2], 0.0, [hTs[0][k]])
                  dst_ap, dst_b = (outT, B_out) if last else (xb, B_xb)
                  nfb = S_ // FB

                  sqF = [kb.sb(es, "sqF", [128, FB], BF16) for _ in range(8)]

                  def stageA_ld(i):
                      kb.load_x(xa, i * FB, FB, xblks[i % 2])

                  def stageA_sq(i):
                      xb_, hT_ = xblks[i % 2], hTs[i % 2]
                      if i > 0:
                          hp = hTs[(i - 1) % 2]
                          for k in range(8):
                              kb.cp("pool", hT_[k][:, 0:2], hp[k][:, FB:FB + 2], [hp[k]], [hT_[k]])
                      kb.rms_sq([xb_[:, k, 0:FB] for k in range(8)], [xb_] * 8, 128, FB, sqF)

                  def stageA_fin(i):
                      xb_, hT_ = xblks[i % 2], hTs[i % 2]
                      rstd = kb.rms_fin(8, 128, D, FB, sqF)
                      for k in range(8):
                          kb.stt(hT_[k][:, 2:2 + FB], xb_[:, k, 0:FB], par[:, 16 + k:17 + k], rstd[:, 0:FB],
                                 ALU.mult, ALU.mult, [xb_, par, rstd], [hT_[k]])

                  def stageB(i, hook=None):
                      hT_ = hTs[i % 2]
                      for r in range(22):
                          if hook is not None:
                              hook(r)
                          ys = []
                          for which in range(2):
                              ch = r + 22 * which
                              ps = kb.nxt("pup", ps_up)
                              for k in range(8):
                                  wg = wugrp[ch // 11]
                                  lc = (ch % 11) * 128
                                  kb.mm(ps[:, 0:FB + 2], wg[:, k, lc:lc + 128], hT_[k][:, 0:FB + 2], k == 0, k == 7,
                                        [wg, hT_[k]], [ps])
                              y = kb.nxt("yb", yb)
                              pc = 60 + ch * 4
                              kb.act(y[:, :], ps[:, 2:FB + 2], AF.Identity, [ps, par], [y], scale=par[:, pc + 2:pc + 3],
                                     bias=par[:, pc + 3:pc + 4])
                              kb.stt(y[:, :], ps[:, 1:FB + 1], par[:, pc + 1:pc + 2], y[:, :], ALU.mult, ALU.add, [ps, par, y], [y])
                              kb.stt(y[:, :], ps[:, 0:FB], par[:, pc:pc + 1], y[:, :], ALU.mult, ALU.add, [ps, par, y], [y])
                              ys.append(y)
                          g_ = kb.nxt("gl", gl)
                          kb.act(g_[:, :], ys[0][:, :], AF.Gelu_apprx_tanh, [ys[0]], [g_])
                          kb.tt("pool", aT[r][:, :], g_[:, :], ys[1][:, :], ALU.mult, [g_, ys[1]], [aT[r]])

                  def stageC(i):
                      c0 = i * FB
                      xb_ = xblks[i % 2]
                      for m in range(8):
                          ps = kb.nxt("pdn", ps_dn)
                          for r in range(22):
                              kb.mm(ps[:, 0:FB], wd[:, r, m * 128:(m + 1) * 128], aT[r][:, :], r == 0, r == 21, [wd, aT[r]], [ps])
                          kb.cp("act", yT[m][:, :], ps[:, 0:FB], [ps], [yT[m]])

                  def stageCt_sq(i):
                      kb.rms_sq([yT[m][:, :] for m in range(8)], yT, 128, FB, sqF)

                  def stageCt_fin(i):
                      c0 = i * FB
                      xb_ = xblks[i % 2]
                      rstd = kb.rms_fin(8, 128, D, FB, sqF)
                      for m in range(8):
                          kb.stt(yT[m][:, :], yT[m][:, :], par[:, 24 + m:25 + m], rstd[:, 0:FB], ALU.mult, ALU.mult,
                                 [yT[m], par, rstd], [yT[m]])
                          kb.tt("pool", yT[m][:, :], yT[m][:, :], xb_[:, m, :], ALU.add, [yT[m], xb_], [yT[m]])
                          kb.dma("sp", dst_ap[m * 128:(m + 1) * 128, c0:c0 + FB], yT[m][:, :], dst_b, [yT[m]])

                  stageA_ld(0)
                  stageA_sq(0)
                  stageA_fin(0)
                  for i in range(nfb):
                      def hook(r, i=i):
                          if r == 1 and i > 0:
                              stageCt_sq(i - 1)
                          if r == 5:
                              if i > 0:
                                  stageCt_fin(i - 1)
                              if i + 1 < nfb:
                                  stageA_ld(i + 1)
                          if r == 10 and i + 1 < nfb:
                              stageA_sq(i + 1)
                          if r == 15 and i + 1 < nfb:
                              stageA_fin(i + 1)
                      stageB(i, hook)
                      stageC(i)
                  stageCt_sq(nfb - 1)
                  stageCt_fin(nfb - 1)

        except _Stop:
            S.barrier()
            with CleanStack() as es:
                tcp = kb.sb(es, "tcp", [128, 8, 512], F32)
                kb.dma("sp", tcp[:, :, :], xT.rearrange("(k p) s -> p k s", p=128)[:, :, 0:512], tcp)
                kb.dma("sp", outT.rearrange("(k p) s -> p k s", p=128)[:, :, 0:512], tcp[:, :, :], B_out, [tcp])
        S.wait_all("sp", [B_out])
        S.barrier()
        keys = ["pe", "act", "dve", "pool"] + list(S.dma_bufs.keys())
        sems = {k: top.enter_context(nc.semaphore(f"s{j}")) for j, k in enumerate(keys)}
        kb.nsems = len(keys)
        with nc.Block() as block:
            @block.tensor
            def _(e):
                S.emit_one(e, "pe", sems)

            @block.scalar
            def _(e):
                S.emit_one(e, "act", sems)

            @block.vector
            def _(e):
                S.emit_one(e, "dve", sems)

            @block.gpsimd
            def _(e):
                S.emit_one(e, "pool", sems)

            @block.sync
            def _(e):
                S.emit_one(e, "sp", sems)
    return nc, kb


def _t5_bucket(dist):
    d = np.maximum(dist, 0)
    lr = np.log(np.maximum(d, 1).astype(np.float32) / np.float32(16)) / np.float32(math.log(128 / 16))
    large = 16 + (lr * 16).astype(np.int32)
    large = np.minimum(large, 31)
    return np.where(d < 16, d, large)


def host_prep(inp, S_, depth):
    f = np.float32
    L = depth
    w_in = np.asarray(inp["w_in"], f)[:L]
    A0, Fo, M0 = 0, 768, 1540
    cols1 = np.concatenate([
        np.arange(0, 512),
        np.arange(512, 576), np.arange(512, 576), np.arange(576, 640), np.arange(576, 640),
        np.arange(640, 768), Fo + np.arange(512, 768),
        Fo + np.arange(0, 256), Fo + np.arange(256, 512), Fo + np.arange(768, 772)])
    assert cols1.size == W1C
    wa1 = np.ascontiguousarray(w_in[:, :, cols1])
    kr = M0 + 384 + np.arange(32)
    krp = np.concatenate([kr[16:], kr[:16]])
    dmy = M0 + np.arange(64)
    cols2 = np.concatenate([M0 + np.arange(0, 384), dmy, kr, dmy, krp])
    assert cols2.size == W2C
    wa2 = np.ascontiguousarray(w_in[:, :, cols2])
    w_uq = np.asarray(inp["w_uq"], f)[:L]
    cq_ = []
    for h in range(4):
        cq_.append(np.arange(h * 96, (h + 1) * 96))
    for h in range(4):
        rp = h * 96 + 64 + np.arange(32)
        cq_.append(np.concatenate([np.arange(h * 96, h * 96 + 64), rp[16:], rp[:16]]))
    wuq = np.ascontiguousarray(w_uq[:, :, np.concatenate(cq_)])
    w_ukv = np.asarray(inp["w_ukv"], f)[:L]
    ck = np.concatenate([h * 128 + np.arange(64) for h in range(4)] + [h * 128 + 64 + np.arange(64) for h in range(4)])
    wukv = np.ascontiguousarray(w_ukv[:, :, ck])
    params = np.zeros((L, 128, NPAR), f)

    def pc(v):
        return v.reshape(L, -1, 128).transpose(0, 2, 1)

    params[:, :, 0:8] = pc(np.asarray(inp["attn_pre_norm"], f)[:L])
    params[:, :, 8:16] = pc(np.asarray(inp["attn_post_norm"], f)[:L])
    params[:, :, 16:24] = pc(np.asarray(inp["ffn_pre_norm"], f)[:L])
    params[:, :, 24:32] = pc(np.asarray(inp["ffn_post_norm"], f)[:L])
    gn = np.asarray(inp["group_norm"], f)[:L].reshape(L, 16, 64).transpose(0, 2, 1)
    params[:, 0:64, 32:48] = gn
    params[:, 64:128, 32:48] = gn
    params[:, :, 48:50] = pc(np.asarray(inp["q_latent_norm"], f)[:L])
    params[:, :, 50:51] = pc(np.asarray(inp["kv_latent_norm"], f)[:L])
    params[:, 0:4, 51] = np.asarray(inp["forget_bias"], f)[:L]
    params[:, :, 52:60] = np.asarray(inp["swa_sinks"], f)[:L][:, None, :]
    cw = np.asarray(inp["conv_w"], f)[:L]
    cb = np.asarray(inp["conv_b"], f)[:L]
    cv = np.concatenate([cw, cb[:, None, :]], axis=1)
    cv = cv.reshape(L, 4, 44, 128).transpose(0, 3, 2, 1)
    params[:, :, 60:] = cv.reshape(L, 128, 176)
    ident = np.eye(128, dtype=f)
    s_ = np.arange(128)[:, None]
    t_ = np.arange(128)[None, :]
    maskT = np.where(s_ <= t_, 0.0, -30000.0).astype(f)
    sel = np.zeros((4, 12, 128), f)
    for h in range(4):
        for p in range(3):
            sel[h, h * 3 + p, 64 + p] = 1.0
            sel[h, h * 3 + p, 96 + p] = 1.0
    pos = np.arange(S_, dtype=f)
    inv = (np.float32(10000.0) ** (-(np.arange(16, dtype=f) * np.float32(2.0) / np.float32(32)))).astype(f)
    ang = (pos[:, None] * inv[None, :]).astype(f)
    cs, sn = np.cos(ang).astype(f).T, np.sin(ang).astype(f).T
    rope = np.zeros((2, 128, S_), f)
    rope[0, 64:80] = cs
    rope[0, 80:96] = cs
    rope[1, 64:80] = -sn
    rope[1, 80:96] = sn
    rel = np.asarray(inp["rel_bias"], f)
    biasg = np.zeros((128, 2, 2, 4, 128), f)
    m01 = np.zeros((128, 2, 2, 4, 128), f)
    q_ = np.arange(128)[None, :]
    for pcx in range(2):
        dist = (q_ + 128 - s_) if pcx == 0 else (q_ - s_)
        ok = (dist >= 0) & (dist < 128)
        bk = _t5_bucket(np.clip(dist, 0, 127))
        for g in range(2):
            for hl in range(4):
                biasg[:, g, pcx, hl, :] = rel[bk, 4 * g + hl]
                m01[:, g, pcx, hl, :] = ok.astype(f)
    common = dict(wa1=wa1, wa2=wa2, wuq=wuq, wukv=wukv,
                  wout=np.ascontiguousarray(np.asarray(inp["w_out"], f)[:L]),
                  wup=np.ascontiguousarray(np.asarray(inp["w_up"], f)[:L]),
                  wdn=np.ascontiguousarray(np.asarray(inp["w_down"], f)[:L]),
                  params=params, c_ident=ident, c_mask=maskT, c_sel=sel.reshape(4, 12 * 128),
                  c_rope=rope, c_biasg=biasg.reshape(128, 2048), c_m01=m01.reshape(128, 2048))
    return common


_CACHE = {}


def run(inp, nblk=8, depth=DEPTH, ncores=8):
    S_ = nblk * 512
    key = (nblk, depth)
    if key not in _CACHE:
        _CACHE[key] = build_program(nblk, depth)[0]
    nc = _CACHE[key]
    common = host_prep(inp, S_, depth)
    x = np.asarray(inp["x"], np.float32)
    in_maps = []
    for b in range(ncores):
        m = dict(common)
        m["xT"] = np.ascontiguousarray(x[b, :S_, :].T)
        in_maps.append(m)
    res = run_bass_kernel_spmd(nc, in_maps, core_ids=list(range(ncores)))
    out = np.stack([np.ascontiguousarray(r["outT"].T) for r in res.results], axis=0)
    return out.astype(np.float32)


def kernel(**inputs):
    return run(inputs, nblk=SEQ // 512, depth=DEPTH, ncores=8)
```
